# Optimizing a Trainium2 kernel written in Bass

```python
import jax
import jax.numpy as jnp
from jax import lax
import numpy as np

D_MODEL = 2048
BATCH = 32
SEQ = 256
DEPTH = 2
DEC_BATCH = 2
DEC_SEQ = 4096
PAST_LEN = 256

GRID_W = 64
MIX_W = D_MODEL
GROUP_W = MIX_W // 4
N_POOL_GROUPS = 4
POOL_WINDOWS = (2, 4, 8, 16)
POOL_GW = GROUP_W // N_POOL_GROUPS
HG_HEADS = 4
HG_DK = GROUP_W // HG_HEADS
HG_DV = HG_DK
HG_CHUNK = 16
ATT_HEADS = 8
ATT_KV_HEADS = 2
ATT_GROUP = ATT_HEADS // ATT_KV_HEADS
HEAD_DIM = GROUP_W // ATT_HEADS
WINDOW = 128
BLOCK = 128
ROPE_BASE = 10000.0
S5_CH = 16
S5_GROUPS = GROUP_W // S5_CH
S5_N = 64
FFN_DIM = 5632
CONV_W = 3
EPS = 1e-6
NEG_INF = -1e30

OFF_POOL = 0
OFF_HG_Q = OFF_POOL + GROUP_W
OFF_HG_FF = OFF_HG_Q + GROUP_W
OFF_HG_FB = OFF_HG_FF + GROUP_W
OFF_HG_I = OFF_HG_FB + GROUP_W
OFF_HG_G = OFF_HG_I + GROUP_W
OFF_ATT_Q = OFF_HG_G + GROUP_W
OFF_ATT_K = OFF_ATT_Q + ATT_HEADS * HEAD_DIM
OFF_ATT_V = OFF_ATT_K + ATT_KV_HEADS * HEAD_DIM
OFF_S5 = OFF_ATT_V + ATT_KV_HEADS * HEAD_DIM
IN_COLS = OFF_S5 + GROUP_W

kernel_name = "hybrid_pool_hgrn2_swa_s5_prefix_dit_step"


def rms_norm(x, g):
    xf = x.astype(jnp.float32)
    y = xf * lax.rsqrt(jnp.mean(xf * xf, axis=-1, keepdims=True) + EPS)
    return y * g.astype(jnp.float32)


def centred_mean(x, w):
    T = x.shape[1]
    cs = jnp.concatenate([jnp.zeros_like(x[:, :1]), jnp.cumsum(x, axis=1)], axis=1)
    t = jnp.arange(T)
    lo = jnp.clip(t - w // 2, 0, T)
    hi = jnp.clip(t + w // 2, 0, T)
    cnt = (hi - lo).astype(x.dtype)
    return (cs[:, hi] - cs[:, lo]) / cnt[None, :, None]


def pool_mixer(u, w_lin, scale):
    B, T, _ = u.shape
    ug = u.astype(jnp.float32).reshape(B, T, N_POOL_GROUPS, POOL_GW)
    pooled = jnp.stack([centred_mean(ug[:, :, gi], w) for gi, w in enumerate(POOL_WINDOWS)], axis=2) - ug
    y = jnp.einsum('btgc,gce->btge', pooled, w_lin.astype(jnp.float32))
    return y.reshape(B, T, GROUP_W) * scale.astype(jnp.float32)


def hgrn_chunk_scan(q, k, v, logf, s0):
    B, T, H, K = q.shape
    V = v.shape[-1]
    C = HG_CHUNK
    n = T // C
    rs = lambda a: a.reshape(B, n, C, H, a.shape[-1])
    qc, kc, vc, gc = rs(q), rs(k), rs(v), rs(logf)
    b = jnp.cumsum(gc, axis=2)
    mask = jnp.tril(jnp.ones((C, C), bool))[None, None, :, :, None, None]
    dec = jnp.exp(jnp.where(mask, b[:, :, :, None] - b[:, :, None, :], -jnp.inf))
    att = jnp.einsum('bnthk,bntshk,bnshk->bnths', qc, dec, kc)
    intra = jnp.einsum('bnths,bnshv->bnthv', att, vc)
    b_last = b[:, :, -1]
    k_dec = kc * jnp.exp(b_last[:, :, None] - b)
    chunk_kv = jnp.einsum('bnshk,bnshv->bnhkv', k_dec, vc)
    decay_last = jnp.exp(b_last)

    def step(S, inp):
        dl, kv = inp
        return dl[..., None] * S + kv, S

    s_final, s_in = lax.scan(step, s0.astype(jnp.float32),
                             (jnp.moveaxis(decay_last, 1, 0), jnp.moveaxis(chunk_kv, 1, 0)))
    s_in = jnp.moveaxis(s_in, 0, 1)
    inter = jnp.einsum('bnthk,bnhkv->bnthv', qc * jnp.exp(b), s_in)
    return (intra + inter).reshape(B, T, H, V), s_final


def hgrn_mixer(q, ff, fb, i, g, lb, s0, norm_g):
    B, T, _ = q.shape
    heads = lambda a: a.astype(jnp.float32).reshape(B, T, HG_HEADS, HG_DK)
    qh, vh = heads(q), heads(i)
    o = jnp.zeros((B, T, HG_HEADS, HG_DV), jnp.float32)
    finals = []
    for d, fx in enumerate((ff, fb)):
        l = lb[d]
        logf = heads(jnp.logaddexp(jnp.log(l), jnp.log1p(-l) + jax.nn.log_sigmoid(fx.astype(jnp.float32))))
        kh = -jnp.expm1(logf)
        if d == 0:
            od, sd = hgrn_chunk_scan(qh, kh, vh, logf, s0[:, 0])
        else:
            fl = lambda a: jnp.flip(a, axis=1)
            od, sd = hgrn_chunk_scan(fl(qh), fl(kh), fl(vh), fl(logf), s0[:, 1])
            od = fl(od)
        o = o + od
        finals.append(sd)
    o = rms_norm(o, norm_g).reshape(B, T, GROUP_W) * jax.nn.silu(g.astype(jnp.float32))
    return o, jnp.stack(finals, axis=1)


def axial_rope(x, rows, cols):
    half = HEAD_DIM // 2
    nf = half // 2
    inv = ROPE_BASE ** (-jnp.arange(nf, dtype=jnp.float32) / nf)

    def rot(xa, pos):
        ang = pos.astype(jnp.float32)[:, None] * inv[None]
        cos, sin = jnp.cos(ang)[None, :, None], jnp.sin(ang)[None, :, None]
        x1, x2 = xa[..., :nf], xa[..., nf:]
        return jnp.concatenate([x1 * cos - x2 * sin, x1 * sin + x2 * cos], axis=-1)

    return jnp.concatenate([rot(x[..., :half], rows), rot(x[..., half:], cols)], axis=-1)


def attend_context(q, k, v, sink):
    B, L = q.shape[:2]
    qg = q.reshape(B, L, ATT_KV_HEADS, ATT_GROUP, HEAD_DIM)
    s = jnp.einsum('blkgd,bmkd->bkglm', qg, k) * HEAD_DIM ** -0.5
    sk = jnp.broadcast_to(sink.astype(jnp.float32).reshape(1, ATT_KV_HEADS, ATT_GROUP, 1, 1), s.shape[:-1] + (1,))
    p = jax.nn.softmax(jnp.concatenate([s, sk], axis=-1), axis=-1)[..., :L]
    o = jnp.einsum('bkglm,bmkd->blkgd', p, v)
    return o.reshape(B, L, GROUP_W)


def attend_latent(q, k, v, kc, vc, sink):
    B, T = q.shape[:2]
    L = kc.shape[1]
    nb = T // BLOCK
    S = 3 * BLOCK
    scale = HEAD_DIM ** -0.5
    qb = q.reshape(B, nb, BLOCK, ATT_KV_HEADS, ATT_GROUP, HEAD_DIM)
    pad = ((0, 0), (BLOCK, BLOCK), (0, 0), (0, 0))
    idx = jnp.arange(nb)[:, None] * BLOCK + jnp.arange(S)[None, :]
    kb = jnp.pad(k, pad)[:, idx]
    vb = jnp.pad(v, pad)[:, idx]
    qpos = jnp.arange(T).reshape(nb, BLOCK)
    kpos = idx - BLOCK
    valid = ((kpos[:, None, :] >= 0) & (kpos[:, None, :] < T)
             & (jnp.abs(qpos[:, :, None] - kpos[:, None, :]) <= WINDOW))
    s_loc = jnp.einsum('bnqkgd,bnskd->bnkgqs', qb, kb) * scale
    s_loc = jnp.where(valid[None, :, None, None], s_loc, NEG_INF)
    s_ctx = jnp.einsum('bnqkgd,bmkd->bnkgqm', qb, kc.astype(jnp.float32)) * scale
    sk = jnp.broadcast_to(sink.astype(jnp.float32).reshape(1, 1, ATT_KV_HEADS, ATT_GROUP, 1, 1), s_loc.shape[:-1] + (1,))
    p = jax.nn.softmax(jnp.concatenate([s_loc, s_ctx, sk], axis=-1), axis=-1)
    o = (jnp.einsum('bnkgqs,bnskd->bnqkgd', p[..., :S], vb)
         + jnp.einsum('bnkgqm,bmkd->bnqkgd', p[..., S:S + L], vc.astype(jnp.float32)))
    return o.reshape(B, T, GROUP_W)


def _lin_combine(left, right):
    a1, b1 = left
    a2, b2 = right
    return a2 * a1, a2 * b1 + b2


def s5_discretise(a_re, a_im, log_dt, b_re, b_im):
    lam = lax.complex(a_re.astype(jnp.float32), a_im.astype(jnp.float32))
    dt = jnp.exp(log_dt.astype(jnp.float32))[:, None]
    lam_bar = jnp.exp(lam * dt)
    b = lax.complex(b_re.astype(jnp.float32), b_im.astype(jnp.float32))
    b_bar = ((lam_bar - 1.0) / lam)[..., None] * b
    return lam_bar, b_bar


def s5_scan(u, lam_bar, b_bar, h0):
    bu = jnp.einsum('gnc,btgc->btgn', b_bar, u.astype(jnp.complex64))
    bu = bu.at[:, 0].add(lam_bar * h0)
    a = jnp.broadcast_to(lam_bar, bu.shape)
    _, hs = lax.associative_scan(_lin_combine, (a, bu), axis=1)
    return hs


def s5_mixer(u, p, h0):
    B, T, _ = u.shape
    ug = u.astype(jnp.float32).reshape(B, T, S5_GROUPS, S5_CH)
    out = p['s5_d'].astype(jnp.float32).reshape(S5_GROUPS, S5_CH) * ug
    finals = []
    for d in range(2):
        lam_bar, b_bar = s5_discretise(p['s5_a_re'][d], p['s5_a_im'][d], p['s5_log_dt'][d],
                                       p['s5_b_re'][d], p['s5_b_im'][d])
        src = ug if d == 0 else jnp.flip(ug, axis=1)
        hs = s5_scan(src, lam_bar, b_bar, h0[:, d])
        finals.append(hs[:, -1])
        if d == 1:
            hs = jnp.flip(hs, axis=1)
        c_mat = lax.complex(p['s5_c_re'][d].astype(jnp.float32), p['s5_c_im'][d].astype(jnp.float32))
        out = out + jnp.real(jnp.einsum('gcn,btgn->btgc', c_mat, hs))
    y = jax.nn.gelu(out.reshape(B, T, GROUP_W))
    zz = jnp.einsum('btc,ce->bte', y, p['s5_w_glu'].astype(jnp.float32))
    return zz[..., :GROUP_W] * jax.nn.sigmoid(zz[..., GROUP_W:]), jnp.stack(finals, axis=1)


def mixers(h, p, lb, ctx, pos):
    B, T, _ = h.shape
    cols = jnp.einsum('btd,de->bte', h, p['w_in'].astype(jnp.float32))
    sl = lambda off, n: cols[..., off:off + n]
    y_pool = pool_mixer(sl(OFF_POOL, GROUP_W), p['pool_w'], p['pool_scale'])
    hg0 = jnp.zeros((B, 2, HG_HEADS, HG_DK, HG_DV), jnp.float32) if ctx is None else ctx['hgrn'].astype(jnp.float32)
    y_hg, s_hg = hgrn_mixer(sl(OFF_HG_Q, GROUP_W), sl(OFF_HG_FF, GROUP_W), sl(OFF_HG_FB, GROUP_W),
                            sl(OFF_HG_I, GROUP_W), sl(OFF_HG_G, GROUP_W), lb, hg0, p['hg_norm_g'])
    q = rms_norm(sl(OFF_ATT_Q, ATT_HEADS * HEAD_DIM).reshape(B, T, ATT_HEADS, HEAD_DIM), p['q_norm_g'])
    k = rms_norm(sl(OFF_ATT_K, ATT_KV_HEADS * HEAD_DIM).reshape(B, T, ATT_KV_HEADS, HEAD_DIM), p['k_norm_g'])
    v = sl(OFF_ATT_V, ATT_KV_HEADS * HEAD_DIM).reshape(B, T, ATT_KV_HEADS, HEAD_DIM)
    if ctx is None:
        y_att = attend_context(q, k, v, p['att_sink'])
        s5_0 = jnp.zeros((B, 2, S5_GROUPS, S5_N), jnp.complex64)
    else:
        rows, cols_ = pos
        y_att = attend_latent(axial_rope(q, rows, cols_), axial_rope(k, rows, cols_), v,
                              ctx['k'], ctx['v'], p['att_sink'])
        s5_0 = ctx['s5']
    y_s5, s_s5 = s5_mixer(sl(OFF_S5, GROUP_W), p, s5_0)
    y = jnp.einsum('btc,cd->btd', jnp.concatenate([y_pool, y_hg, y_att, y_s5], axis=-1),
                   p['w_out'].astype(jnp.float32))
    new_ctx = (k, v, s_hg, s_s5) if ctx is None else None
    return y, new_ctx


def conv_ffn(h, w_up, conv_w, conv_b, w_down):
    T = h.shape[1]
    u = jnp.einsum('btd,df->btf', h, w_up.astype(jnp.float32))
    a, b = u[..., :FFN_DIM], u[..., FFN_DIM:]
    half = CONV_W // 2
    ap = jnp.pad(a, ((0, 0), (half, half), (0, 0)))
    cw = conv_w.astype(jnp.float32)
    a = ap[:, 0:T] * cw[0] + ap[:, 1:T + 1] * cw[1] + ap[:, 2:T + 2] * cw[2] + conv_b.astype(jnp.float32)
    return jnp.einsum('btf,fd->btd', jax.nn.silu(a) * b, w_down.astype(jnp.float32))


def trunk_layer(x, mod, p, lb, ctx, pos):
    sh1, sc1, g1, sh2, sc2, g2 = [mod[:, j][:, None, :] for j in range(6)]
    h = rms_norm(x, p['norm1_g']) * (1.0 + sc1) + sh1
    y, new_ctx = mixers(h, p, lb, ctx, pos)
    x = x + (g1 * y).astype(x.dtype)
    h2 = rms_norm(x, p['norm2_g']) * (1.0 + sc2) + sh2
    x = x + (g2 * conv_ffn(h2, p['ffn_w_up'], p['ffn_conv_w'], p['ffn_conv_b'], p['ffn_w_down'])).astype(x.dtype)
    return x, new_ctx


def setup_inputs(seed: int = 0) -> dict:
    key = jax.random.key(seed)
    ks = iter(jax.random.split(key, 48))
    f32 = jnp.float32

    def nrm(shape, s=1.0):
        return jax.random.normal(next(ks), shape, f32) * s

    n_idx = jnp.arange(S5_N, dtype=f32)
    s5_shape = (DEPTH, 2, S5_GROUPS, S5_N)
    return {
        'x_prompt': nrm((BATCH, SEQ, D_MODEL)),
        'x_sample': nrm((DEC_BATCH, DEC_SEQ, D_MODEL)),
        'cache_k': nrm((DEC_BATCH, DEPTH, PAST_LEN, ATT_KV_HEADS, HEAD_DIM)),
        'cache_v': nrm((DEC_BATCH, DEPTH, PAST_LEN, ATT_KV_HEADS, HEAD_DIM)),
        'state_hgrn': nrm((DEC_BATCH, DEPTH, 2, HG_HEADS, HG_DK, HG_DV), 0.5),
        'state_s5_re': nrm((DEC_BATCH, DEPTH, 2, S5_GROUPS, S5_N), 0.5),
        'state_s5_im': nrm((DEC_BATCH, DEPTH, 2, S5_GROUPS, S5_N), 0.5),
        'c': nrm((DEC_BATCH, D_MODEL)),
        'c_ctx': nrm((D_MODEL,)),
        'norm1_g': 1.0 + nrm((DEPTH, D_MODEL), 0.02),
        'norm2_g': 1.0 + nrm((DEPTH, D_MODEL), 0.02),
        'ada_w': nrm((DEPTH, D_MODEL, 6 * D_MODEL), 0.5 * D_MODEL ** -0.5),
        'ada_b': nrm((DEPTH, 6 * D_MODEL), 0.02),
        'w_in': nrm((DEPTH, D_MODEL, IN_COLS), D_MODEL ** -0.5),
        'w_out': nrm((DEPTH, MIX_W, D_MODEL), MIX_W ** -0.5),
        'pool_w': nrm((DEPTH, N_POOL_GROUPS, POOL_GW, POOL_GW), POOL_GW ** -0.5),
        'pool_scale': 1.0 + nrm((DEPTH, GROUP_W), 0.02),
        'hg_lb_raw': nrm((DEPTH, 2, GROUP_W)),
        'hg_norm_g': 1.0 + nrm((DEPTH, HG_DV), 0.02),
        'q_norm_g': 1.0 + nrm((DEPTH, HEAD_DIM), 0.02),
        'k_norm_g': 1.0 + nrm((DEPTH, HEAD_DIM), 0.02),
        'att_sink': nrm((DEPTH, ATT_HEADS), 0.5),
        's5_a_re': -0.5 + nrm(s5_shape, 0.02),
        's5_a_im': jnp.pi * n_idx + nrm(s5_shape, 0.02),
        's5_log_dt': jax.random.uniform(next(ks), (DEPTH, 2, S5_GROUPS), f32, np.log(1e-3), np.log(1e-1)),
        's5_b_re': nrm((DEPTH, 2, S5_GROUPS, S5_N, S5_CH), (2 * S5_CH) ** -0.5),
        's5_b_im': nrm((DEPTH, 2, S5_GROUPS, S5_N, S5_CH), (2 * S5_CH) ** -0.5),
        's5_c_re': nrm((DEPTH, 2, S5_GROUPS, S5_CH, S5_N), (2 * S5_N) ** -0.5),
        's5_c_im': nrm((DEPTH, 2, S5_GROUPS, S5_CH, S5_N), (2 * S5_N) ** -0.5),
        's5_d': nrm((DEPTH, GROUP_W), 0.5),
        's5_w_glu': nrm((DEPTH, GROUP_W, 2 * GROUP_W), GROUP_W ** -0.5),
        'ffn_w_up': nrm((DEPTH, D_MODEL, 2 * FFN_DIM), D_MODEL ** -0.5),
        'ffn_conv_w': nrm((DEPTH, CONV_W, FFN_DIM), CONV_W ** -0.5),
        'ffn_conv_b': nrm((DEPTH, FFN_DIM), 0.01),
        'ffn_w_down': nrm((DEPTH, FFN_DIM, D_MODEL), FFN_DIM ** -0.5),
    }


def reference(x_prompt, x_sample, cache_k, cache_v, state_hgrn, state_s5_re, state_s5_im, c, c_ctx,
              norm1_g, norm2_g, ada_w, ada_b, w_in, w_out, pool_w, pool_scale, hg_lb_raw, hg_norm_g,
              q_norm_g, k_norm_g, att_sink, s5_a_re, s5_a_im, s5_log_dt, s5_b_re, s5_b_im, s5_c_re,
              s5_c_im, s5_d, s5_w_glu, ffn_w_up, ffn_conv_w, ffn_conv_b, ffn_w_down):
    f32 = jnp.float32
    T = x_sample.shape[1]
    n_rows = T // GRID_W
    t_idx = jnp.arange(n_rows * GRID_W)
    rows, cols = t_idx // GRID_W, t_idx % GRID_W
    lb_cum = jnp.cumsum(jax.nn.softmax(hg_lb_raw.astype(f32), axis=0), axis=0)
    hg_lb = lb_cum - lb_cum[:1]
    silu_ctx = jax.nn.silu(c_ctx.astype(f32))[None]
    silu_c = jax.nn.silu(c.astype(f32))
    y, z = x_prompt, x_sample
    ks_, vs_, hs_, s5r_, s5i_ = [], [], [], [], []
    for l in range(DEPTH):
        p = {
            'norm1_g': norm1_g[l], 'norm2_g': norm2_g[l], 'w_in': w_in[l], 'w_out': w_out[l],
            'pool_w': pool_w[l], 'pool_scale': pool_scale[l], 'hg_norm_g': hg_norm_g[l],
            'q_norm_g': q_norm_g[l], 'k_norm_g': k_norm_g[l], 'att_sink': att_sink[l],
            's5_a_re': s5_a_re[l], 's5_a_im': s5_a_im[l], 's5_log_dt': s5_log_dt[l],
            's5_b_re': s5_b_re[l], 's5_b_im': s5_b_im[l], 's5_c_re': s5_c_re[l], 's5_c_im': s5_c_im[l],
            's5_d': s5_d[l], 's5_w_glu': s5_w_glu[l], 'ffn_w_up': ffn_w_up[l],
            'ffn_conv_w': ffn_conv_w[l], 'ffn_conv_b': ffn_conv_b[l], 'ffn_w_down': ffn_w_down[l],
        }
        aw, ab = ada_w[l].astype(f32), ada_b[l].astype(f32)
        mod_ctx = (silu_ctx @ aw + ab).reshape(1, 6, D_MODEL)
        mod_lat = (silu_c @ aw + ab).reshape(-1, 6, D_MODEL)
        y, (k_l, v_l, hg_l, s5_l) = trunk_layer(y, mod_ctx, p, hg_lb[l], None, None)
        ks_.append(k_l)
        vs_.append(v_l)
        hs_.append(hg_l)
        s5r_.append(jnp.real(s5_l))
        s5i_.append(jnp.imag(s5_l))
        ctx_l = {
            'k': cache_k[:, l], 'v': cache_v[:, l], 'hgrn': state_hgrn[:, l],
            's5': lax.complex(state_s5_re[:, l].astype(f32), state_s5_im[:, l].astype(f32)),
        }
        z, _ = trunk_layer(z, mod_lat, p, hg_lb[l], ctx_l, (rows, cols))
    new_cache_k = jnp.stack(ks_, axis=1)
    new_cache_v = jnp.stack(vs_, axis=1)
    new_state_hgrn = jnp.stack(hs_, axis=1)
    new_state_s5_re = jnp.stack(s5r_, axis=1)
    new_state_s5_im = jnp.stack(s5i_, axis=1)
    return (y, z, new_cache_k, new_cache_v, new_state_hgrn, new_state_s5_re, new_state_s5_im)
```

```python
import numpy as np
import ml_dtypes
import concourse.bass as bass
import concourse.mybir as mybir
from concourse.bass_utils import run_bass_kernel_spmd
from contextlib import ExitStack

F32 = mybir.dt.float32
BF16 = mybir.dt.bfloat16
I32 = mybir.dt.int32
AF = mybir.ActivationFunctionType
ALU = mybir.AluOpType
EPOCH = 16000
NDQ = 12
PI = float(np.pi)

D = 2048
NT = 2048
INC = 4352
FF = 5632
EPS = 1e-6
G4 = [[0, 1, 2, 3], [4, 5, 6, 7]]
G8 = [list(range(8))]


class Buf:
    __slots__ = ("name", "w", "rs")

    def __init__(self, name=""):
        self.name = name
        self.w = None
        self.rs = []


class FW:
    def __init__(self, nc, es):
        self.nc = nc
        self.es = es
        self.eng = {"pe": nc.tensor, "act": nc.scalar, "dve": nc.vector, "pool": nc.gpsimd, "sp": nc.sync}
        self.cnt = {k: 0 for k in self.eng}
        self.sems = {k: [] for k in self.eng}
        self.seen = {k: {} for k in self.eng}
        self.dq = {}
        self.dqi = {}
        self.last_tok = {k: None for k in self.eng}
        self.ninstr = 0
        self.cccnt = 0
        self.ccsem = None

    def _newsem(self, name):
        return self.es.enter_context(self.nc.semaphore(name))

    def _wait(self, ek, tok):
        if tok is None:
            return
        sem, val, src = tok
        if src == ek and ek == "pe":
            return
        d = self.seen[ek]
        k = id(sem)
        if d.get(k, 0) >= val:
            return
        d[k] = val
        self.eng[ek].wait_ge(sem, val)

    def _deps(self, ek, reads, writes):
        for b in reads:
            self._wait(ek, b.w)
        for b in writes:
            self._wait(ek, b.w)
            for t in b.rs:
                self._wait(ek, t)

    def _commit(self, tok, reads, writes):
        for b in reads:
            b.rs = [t for t in b.rs if t[0] is not tok[0]]
            b.rs.append(tok)
        for b in writes:
            b.w = tok
            b.rs = []

    def op(self, ek, fn, reads=(), writes=()):
        self._deps(ek, reads, writes)
        c = self.cnt[ek]
        ep = c // EPOCH
        while len(self.sems[ek]) <= ep:
            self.sems[ek].append(self._newsem(f"s_{ek}_{len(self.sems[ek])}"))
        sem = self.sems[ek][ep]
        val = c % EPOCH + 1
        ins = fn(self.eng[ek])
        ins.then_inc(sem, 1)
        self.cnt[ek] = c + 1
        tok = (sem, val, ek)
        self._commit(tok, reads, writes)
        self.last_tok[ek] = tok
        self.ninstr += 1
        return tok

    def dma(self, qk, out, in_, reads=(), writes=(), **kw):
        if qk not in self.dq:
            self.dq[qk] = [[self._newsem(f"d_{qk}_{i}"), 0] for i in range(NDQ)]
            self.dqi[qk] = 0
        i = self.dqi[qk]
        self.dqi[qk] = (i + 1) % NDQ
        slot = self.dq[qk][i]
        sem = slot[0]
        if slot[1] > 0:
            self._wait(qk, (sem, slot[1], None))
        self._deps(qk, reads, writes)
        slot[1] += 16
        ins = self.eng[qk].dma_start(out=out, in_=in_, **kw)
        ins.then_inc(sem, 16)
        tok = (sem, slot[1], None)
        self._commit(tok, reads, writes)
        self.ninstr += 1
        return tok

    def collective(self, groups, in_ap, out_ap):
        ek = "pool"
        if self.ccsem is None:
            self.ccsem = self._newsem("ccsem")
        self.cccnt += 1
        ins = self.nc.gpsimd.collective_compute("AllGather", ALU.bypass, replica_groups=groups,
                                                ins=[in_ap], outs=[out_ap])
        ins.then_inc(self.ccsem)
        self._wait(ek, (self.ccsem, self.cccnt, None))

    def barrier(self):
        toks = [t for t in self.last_tok.values() if t is not None]
        for qk, slots in self.dq.items():
            for sem, cntv in slots:
                if cntv > 0:
                    toks.append((sem, cntv, None))
        if self.ccsem is not None and self.cccnt > 0:
            toks.append((self.ccsem, self.cccnt, None))
        for ek in self.eng:
            for t in toks:
                if t[2] == ek:
                    continue
                self._wait(ek, t)


class Rot:
    def __init__(self, items):
        self.items = items
        self.i = 0

    def next(self):
        it = self.items[self.i]
        self.i = (self.i + 1) % len(self.items)
        return it


def _mk(items):
    off = {}
    o = 0
    for k, n in items:
        off[k] = o
        o += n
    return off, o


SPO, SP_N = _mk([("csT", 48), ("adab", 24), ("selb", 2), ("sel4", 12), ("edge4", 4), ("n1g", 32), ("n2g", 32),
                 ("convw", 264), ("convb", 88), ("pool_scale", 8), ("hg_lbraw", 16), ("hg_ng", 2), ("qng", 2), ("kng", 2),
                 ("sink", 8), ("s5are", 64), ("s5aim", 64), ("s5ldt", 64), ("s5d", 8)])
CT, CT_N = _mk([("ident", 128), ("ones", 128), ("blk64", 128), ("perm", 128), ("cmask", 256), ("bmask", 8),
                ("iota1", 128), ("amask", 512)])


class Rot:
    def __init__(self, items):
        self.items = items
        self.i = 0

    def next(self):
        it = self.items[self.i]
        self.i = (self.i + 1) % len(self.items)
        return it


class Prog:
    def __init__(self, stop=None, skipA=False):
        self.stop = stop
        self.skipA = skipA
        self.nc = bass.Bass("TRN2", target_bir_lowering=False)
        self.es = ExitStack()
        self.fw = FW(self.nc, self.es)
        self.uid = 0
        self.inputs = {}
        self.outputs = {}
        self.evi = 0

    def din(self, name, shape, dt=F32):
        if self.skipA and name in ("w_in_s", "w_out_s", "w_up_s", "w_dn_s", "ada_s"):
            return self.nc.dram_tensor(name, list(shape), dt)
        t = self.nc.dram_tensor(name, list(shape), dt, kind="ExternalInput")
        self.inputs[name] = t
        return t

    def dout(self, name, shape, dt=F32):
        t = self.nc.dram_tensor(name, list(shape), dt, kind="ExternalOutput")
        self.outputs[name] = t
        return t

    def dint(self, name, shape, dt=F32, shared=False):
        if shared:
            return self.nc.dram_tensor(name, list(shape), dt, addr_space="Shared")
        return self.nc.dram_tensor(name, list(shape), dt)

    def T(self, ps, shape, dt=F32, name="t"):
        self.uid += 1
        t = ps.enter_context(self.nc.sbuf_tensor(f"{name}_{self.uid}", list(shape), dt))
        return t, Buf(name)

    def P(self, ps, shape=(128, 512), dt=F32, name="p"):
        self.uid += 1
        t = ps.enter_context(self.nc.psum_tensor(f"{name}_{self.uid}", list(shape), dt))
        return t, Buf(name)

    def rot(self, ps, n, shape, dt=F32, name="r", psum=False):
        return Rot([(self.P if psum else self.T)(ps, shape, dt, name) for _ in range(n)])

    def ev(self):
        self.evi += 1
        return "act" if self.evi % 2 else "dve"

    def copy(self, ek, out, in_, reads, writes):
        if ek == "act":
            return self.fw.op("act", lambda e: e.activation(out=out, in_=in_, func=AF.Copy), reads, writes)
        return self.fw.op(ek, lambda e: e.tensor_copy(out=out, in_=in_), reads, writes)

    def load(self, out, in_, wbuf, q="sp"):
        return self.fw.dma(q, out, in_, writes=[wbuf])

    def store(self, out, in_, rbuf, q="pool"):
        return self.fw.dma(q, out, in_, reads=[rbuf])

    def d2d(self, out, in_, q="sp"):
        return self.fw.dma(q, out, in_)

    def build(self):
        nc, fw = self.nc, self.fw
        op = fw.op
        self.es.enter_context(nc.allow_non_contiguous_dma(reason="small strided parameter / halo tables"))
        NL = 1 if (self.stop is not None and self.stop[-1] == "0") else 2
        self.NL = NL
        xin = self.din("xin", [D, NT])
        w_in_s = self.din("w_in_s", [NL, 256, INC])
        w_out_s = self.din("w_out_s", [NL, 256, D])
        w_up_s = self.din("w_up_s", [NL, 256, 2 * FF])
        w_dn_s = self.din("w_dn_s", [NL, 704, D])
        ada_s = self.din("ada_s", [NL, D, 1536])
        smallp = self.din("smallp", [128, SP_N])
        consts = self.din("consts", [128, CT_N])
        hmask_d = self.din("hmask", [128, 2048])
        rope_d = self.din("rope", [128, 2048])
        pic_d = self.din("pic", [128, 5120])
        pool_w_d = self.din("pool_w", [2, 4, 128, 128])
        glu_d = self.din("s5_glu", [2, 512, 1024])
        s5b_d = self.din("s5b", [2, 2, 2, 32, 16, 64])
        s5c_d = self.din("s5c", [2, 2, 2, 32, 64, 16])
        s5h0_d = self.din("s5h0", [2, 128, 64])
        hg0_d = self.din("hg0", [2, 2, 4, 128, 128])
        ckT_d = self.din("ckT", [2, 2, 128, 256])
        cv_d = self.din("cv", [2, 256, 128])

        yT = self.dout("yT", [D, NT]) if self.stop is None else self.dint("yT_i", [D, NT])
        ck_out = self.dout("ck_out", [2, 4, 128, 256])
        cv_out = self.dout("cv_out", [2, 4, 128, 256])
        hg_out = self.dout("hg_out", [2, 4, 2, 4, 128, 128])
        s5_out = self.dout("s5_out", [2, 4, 128, 64])
        self.dbg = self.stop is not None
        if self.dbg:
            dbg_y = self.dout("dbg_ymix", [D, 512], BF16)
            dbg_c = self.dout("dbg_cols", [INC, 512])
            dbg_x = self.dout("dbg_x", [D, 512])

        wbf = {}
        wfull = {}
        for l in range(2):
            for nm, rows, cols in (("in", 256, INC), ("out", 256, D), ("up", 256, 2 * FF), ("dn", 704, D)):
                wbf[(nm, l)] = self.dint(f"wbf_{nm}{l}", [rows, cols], BF16)
                wfull[(nm, l)] = self.dint(f"wfull_{nm}{l}", [rows * 8, cols], BF16, shared=True)
        modblob = self.dint("modblob", [128, 72])
        modg = self.dint("modg", [1024, 72])
        colsT = self.dint("colsT", [INC, NT])
        ymixT = self.dint("ymixT", [D, NT], BF16)
        xmid = self.dint("xmid", [D, NT])
        x1 = self.dint("x1", [D, NT])
        qnT = self.dint("qnT", [512, NT], BF16)
        knT = self.dint("knT", [128, NT])
        oaccT = self.dint("oaccT", [512, 1024])
        XO, XB = _mk([("pool", 64), ("kh", 256), ("vh", 256), ("hgB", 1024), ("hgA", 8), ("s5", 64)])
        xblob = [self.dint(f"xblob{l}", [128, XB]) for l in range(2)]
        xgat = [self.dint(f"xgat{l}", [512, XB]) for l in range(2)]
        hblob = [self.dint(f"hblob{l}", [128, 32]) for l in range(2)]
        hgat = [self.dint(f"hgat{l}", [512, 32]) for l in range(2)]

        gs = self.es
        sp_t, sp_b = self.T(gs, [128, SP_N], F32, "smallp")
        ct_t, ct_b = self.T(gs, [128, CT_N], F32, "consts")
        self.load(sp_t[:], smallp.ap(), sp_b)
        self.load(ct_t[:], consts.ap(), ct_b)
        identb, identb_b = self.T(gs, [128, 128], BF16, "identb")
        op("dve", lambda e: e.tensor_copy(out=identb[:], in_=ct_t[:, CT["ident"]:CT["ident"] + 128]), [ct_b], [identb_b])
        epsc, epsc_b = self.T(gs, [128, 1], F32, "epsc")
        op("dve", lambda e: e.memset(epsc[:], EPS), [], [epsc_b])
        modt, modt_b = self.T(gs, [128, 2, 2, 96], F32, "modt")
        gm, gm_b = self.T(gs, [128, 2, 2, 2, 16], F32, "gm")

        def SPv(name, off=0, n=1):
            o = SPO[name] + off
            return sp_t[:, o:o + n]

        def CTv(name, off=0, n=1):
            o = CT[name] + off
            return ct_t[:, o:o + n]

        ones_ap = CTv("ones", 0, 128)
        blk64_ap = CTv("blk64", 0, 128)

        def p0_layer(l):
            with ExitStack() as ps:
                stg = self.rot(ps, 3, [128, 2048], F32, "wst")
                stb = self.rot(ps, 3, [128, 2048], BF16, "wsb")
                k = 0
                for nm, src, rows, cols in (("in", w_in_s, 256, INC), ("out", w_out_s, 256, D),
                                            ("up", w_up_s, 256, 2 * FF), ("dn", w_dn_s, 704, D)):
                    for r0 in range(0, rows, 128):
                        rn = min(128, rows - r0)
                        for c0 in range(0, cols, 2048):
                            cn = min(2048, cols - c0)
                            a, ab = stg.next()
                            b, bb = stb.next()
                            self.load(a[0:rn, 0:cn], src.ap()[l, r0:r0 + rn, c0:c0 + cn], ab)
                            ek = ("dve", "pool", "act")[k % 3]
                            k += 1
                            self.copy(ek, b[0:rn, 0:cn], a[0:rn, 0:cn], [ab], [bb])
                            self.store(wbf[(nm, l)].ap()[r0:r0 + rn, c0:c0 + cn], b[0:rn, 0:cn], bb)
                fw.barrier()
            for nm in ("in", "out", "up", "dn"):
                fw.collective(G8, wbf[(nm, l)].ap(), wfull[(nm, l)].ap())
            fw.barrier()

        def p1_mod():
            with ExitStack() as ps:
                sil, sil_b = self.T(ps, [128, 48], F32, "sil")
                op("act", lambda e: e.activation(out=sil[:], in_=SPv("csT", 0, 48), func=AF.Silu), [sp_b], [sil_b])
                modloc, modloc_b = self.T(ps, [128, 72], F32, "modloc")
                op("dve", lambda e: e.memset(modloc[:], 0.0), [], [modloc_b])
                aw = self.rot(ps, 3, [128, 16, 128], F32, "aw")
                pp = self.rot(ps, 2, [128, 512], F32, "pmod", psum=True)
                for l in range(NL):
                    for ql in range(12):
                        a, ab = aw.next()
                        self.load(a[:], ada_s.ap()[l].rearrange("(k p) c -> p k c", p=128)[:, :, ql * 128:(ql + 1) * 128], ab)
                        p_, pb = pp.next()
                        for kc in range(16):
                            op("pe", lambda e, a=a, p_=p_, kc=kc: e.matmul(p_[:, 0:3], lhsT=a[:, kc, :], rhs=sil[:, kc * 3:kc * 3 + 3],
                                                                          start=(kc == 0), stop=(kc == 15)), [ab, sil_b], [pb])
                        outv = modloc[:, l * 36:(l + 1) * 36].rearrange("p (r q) -> p r q", q=12)[:, :, ql]
                        op("dve", lambda e, outv=outv, p_=p_, l=l, ql=ql: e.tensor_scalar(
                            out=outv, in0=p_[:, 0:3], scalar1=SPv("adab", l * 12 + ql, 1), scalar2=None, op0=ALU.add),
                           [pb, sp_b], [modloc_b])
                self.store(modblob.ap(), modloc[:], modloc_b)
                fw.barrier()
                fw.collective(G8, modblob.ap(), modg.ap())
                fw.barrier()
                msb, msb_b = self.T(ps, [128, 8, 72], F32, "msb")
                self.load(msb[:], modg.ap().rearrange("(r p) x -> p r x", p=128), msb_b)
                for l in range(2):
                    def row(r_):
                        return msb[:, :, l * 36 + r_ * 12: l * 36 + r_ * 12 + 12]
                    o_ctx = modt[:, l, 0, :].rearrange("p (r q) -> p r q", q=12)
                    o_lat = modt[:, l, 1, :].rearrange("p (r q) -> p r q", q=12)
                    op("dve", lambda e, o_ctx=o_ctx, row=row: e.tensor_copy(out=o_ctx, in_=row(0)), [msb_b], [modt_b])
                    op("dve", lambda e, o_lat=o_lat, row=row: e.tensor_scalar(out=o_lat, in0=row(1), scalar1=SPv("selb", 0, 1),
                                                                             scalar2=None, op0=ALU.mult), [msb_b, sp_b], [modt_b])
                    op("dve", lambda e, o_lat=o_lat, row=row: e.scalar_tensor_tensor(out=o_lat, in0=row(2), scalar=SPv("selb", 1, 1),
                                                                                     in1=o_lat, op0=ALU.mult, op1=ALU.add),
                       [msb_b, sp_b, modt_b], [modt_b])
                    for kind in range(2):
                        for wh in range(2):
                            sc = modt[:, l, kind, 16 + 48 * wh: 32 + 48 * wh]
                            ng = SPv("n1g" if wh == 0 else "n2g", l * 16, 16)
                            op("dve", lambda e, sc=sc, ng=ng, l=l, kind=kind, wh=wh: e.scalar_tensor_tensor(
                                out=gm[:, l, kind, wh, :], in0=sc, scalar=1.0, in1=ng, op0=ALU.add, op1=ALU.mult),
                               [modt_b, sp_b], [gm_b])
                fw.barrier()

        def norm_tile(xt, xb, n, h, hb, l, kind, wh, sqr, pss, pssb, rstd, rstdb, tmpr):
            for kc in range(16):
                s_, sb_ = sqr.next()
                op("act", lambda e, s_=s_, kc=kc: e.activation(out=s_[:, 0:n], in_=xt[:, kc, 0:n], func=AF.Square), [xb], [sb_])
                op("pe", lambda e, s_=s_, kc=kc: e.matmul(pss[:, 0:n], lhsT=ones_ap, rhs=s_[:, 0:n], start=(kc == 0), stop=(kc == 15)),
                   [sb_, ct_b], [pssb])
            op("act", lambda e: e.activation(out=rstd[:, 0:n], in_=pss[:, 0:n], func=AF.Sqrt, scale=1.0 / D, bias=epsc[:]),
               [pssb, epsc_b], [rstdb])
            op("dve", lambda e: e.reciprocal(out=rstd[:, 0:n], in_=rstd[:, 0:n]), [rstdb], [rstdb])
            shift_off = 0 if wh == 0 else 48
            for kc in range(16):
                t_, tb_ = tmpr.next()
                op("dve", lambda e, t_=t_, kc=kc: e.scalar_tensor_tensor(out=t_[:, 0:n], in0=xt[:, kc, 0:n], scalar=gm[:, l, kind, wh, kc:kc + 1],
                                                                        in1=rstd[:, 0:n], op0=ALU.mult, op1=ALU.mult), [xb, gm_b, rstdb], [tb_])
                op("act", lambda e, t_=t_, kc=kc: e.activation(out=h[:, kc, 0:n], in_=t_[:, 0:n], func=AF.Identity,
                                                               bias=modt[:, l, kind, shift_off + kc: shift_off + kc + 1], scale=1.0),
                   [tb_, modt_b], [hb])

        def phase_A(l, xsrc):
            wf = wfull[("in", l)].ap().rearrange("(k p) c -> p k c", p=128)
            xs = xsrc.rearrange("(k p) t -> p k t", p=128)
            with ExitStack() as ps:
                xr = self.rot(ps, 2, [128, 16, 512], F32, "xa")
                hr = self.rot(ps, 2, [128, 16, 512], BF16, "ha")
                sqr = self.rot(ps, 2, [128, 512], F32, "sq")
                tmpr = self.rot(ps, 2, [128, 512], F32, "tmp")
                rstd, rstdb = self.T(ps, [128, 512], F32, "rstd")
                pss, pssb = self.P(ps, name="pss")
                wr = self.rot(ps, 3, [128, 16, 256], BF16, "wa")
                stg = self.rot(ps, 3, [128, 512], F32, "stga")
                pso = self.rot(ps, 3, [128, 512], F32, "psoa", psum=True)
                for ti in range(4):
                    kind = 0 if ti < 2 else 1
                    xt, xb = xr.next()
                    self.load(xt[:], xs[:, :, ti * 512:(ti + 1) * 512], xb)
                    h, hb = hr.next()
                    norm_tile(xt, xb, 512, h, hb, l, kind, 0, sqr, pss, pssb, rstd, rstdb, tmpr)
                    for sl in range(17):
                        w, wb = wr.next()
                        self.load(w[:], wf[:, :, sl * 256:(sl + 1) * 256], wb)
                        for cc in range(2):
                            p_, pb = pso.next()
                            for kc in range(16):
                                op("pe", lambda e, p_=p_, w=w, h=h, kc=kc, cc=cc: e.matmul(
                                    p_[:], lhsT=w[:, kc, cc * 128:(cc + 1) * 128], rhs=h[:, kc, :], start=(kc == 0), stop=(kc == 15)),
                                   [wb, hb], [pb])
                            s_, sb_ = stg.next()
                            self.copy(self.ev(), s_[:], p_[:], [pb], [sb_])
                            ch = sl * 2 + cc
                            self.store(colsT.ap()[ch * 128:(ch + 1) * 128, ti * 512:(ti + 1) * 512], s_[:], sb_)
                fw.barrier()

        def phase_C(l, xsrc, xdst):
            wf = wfull[("out", l)].ap().rearrange("(k p) c -> p k c", p=128)
            xs = xsrc.rearrange("(k p) t -> p k t", p=128)
            xd = xdst.rearrange("(k p) t -> p k t", p=128)
            ym = ymixT.ap().rearrange("(k p) t -> p k t", p=128)
            with ExitStack() as ps:
                xr = self.rot(ps, 2, [128, 16, 512], F32, "xc")
                yr = self.rot(ps, 2, [128, 16, 512], BF16, "yc")
                wr = self.rot(ps, 3, [128, 16, 256], BF16, "wc")
                stg = self.rot(ps, 3, [128, 512], F32, "stgc")
                pso = self.rot(ps, 3, [128, 512], F32, "psoc", psum=True)
                for ti in range(4):
                    kind = 0 if ti < 2 else 1
                    xt, xb = xr.next()
                    self.load(xt[:], xs[:, :, ti * 512:(ti + 1) * 512], xb)
                    y, yb = yr.next()
                    self.load(y[:], ym[:, :, ti * 512:(ti + 1) * 512], yb)
                    for sl in range(8):
                        w, wb = wr.next()
                        self.load(w[:], wf[:, :, sl * 256:(sl + 1) * 256], wb)
                        for cc in range(2):
                            ch = sl * 2 + cc
                            p_, pb = pso.next()
                            for kc in range(16):
                                op("pe", lambda e, p_=p_, w=w, y=y, kc=kc, cc=cc: e.matmul(
                                    p_[:], lhsT=w[:, kc, cc * 128:(cc + 1) * 128], rhs=y[:, kc, :], start=(kc == 0), stop=(kc == 15)),
                                   [wb, yb], [pb])
                            s_, sb_ = stg.next()
                            op("dve", lambda e, s_=s_, p_=p_, ch=ch, xt=xt, kind=kind: e.scalar_tensor_tensor(
                                out=s_[:], in0=p_[:], scalar=modt[:, l, kind, 32 + ch:33 + ch], in1=xt[:, ch, :], op0=ALU.mult, op1=ALU.add),
                               [pb, modt_b, xb], [sb_])
                            self.store(xd[:, ch, ti * 512:(ti + 1) * 512], s_[:], sb_)
                fw.barrier()

        def phase_D(l, xsrc, xdst):
            wu = wfull[("up", l)].ap().rearrange("(k p) c -> p k c", p=128)
            wd = wfull[("dn", l)].ap().rearrange("(k p) c -> p k c", p=128)
            xs = xsrc.rearrange("(k p) t -> p k t", p=128)
            xd = xdst.rearrange("(k p) t -> p k t", p=128)
            with ExitStack() as ps:
                e_t, e_b = self.T(ps, [128, 16, 2], F32, "edge")
                self.load(e_t[:, :, 0:1], xs[:, :, 1024:1025], e_b)
                self.load(e_t[:, :, 1:2], xs[:, :, 2047:2048], e_b)
                self.store(hblob[l].ap().rearrange("p (k e) -> p k e", e=2), e_t[:], e_b)
                fw.barrier()
                fw.collective(G4, hblob[l].ap(), hgat[l].ap())
                fw.barrier()
            with ExitStack() as ps:
                hg_t, hg_b = self.T(ps, [128, 4, 16, 2], F32, "hgt")
                self.load(hg_t[:], hgat[l].ap().rearrange("(r p) (k e) -> p r k e", p=128, e=2), hg_b)
                xh, xh_b = self.T(ps, [128, 16, 4], F32, "xh")
                self.load(xh[:, :, 1:3], xs[:, :, 1535:1537], xh_b)
                for r in range(4):
                    for col, e_i, selo in ((0, 1, 0), (3, 0, 4)):
                        if r == 0:
                            op("dve", lambda e, col=col, e_i=e_i, selo=selo, r=r: e.tensor_scalar(
                                out=xh[:, :, col], in0=hg_t[:, r, :, e_i], scalar1=SPv("sel4", selo + r, 1), scalar2=None, op0=ALU.mult),
                               [hg_b, sp_b], [xh_b])
                        else:
                            op("dve", lambda e, col=col, e_i=e_i, selo=selo, r=r: e.scalar_tensor_tensor(
                                out=xh[:, :, col], in0=hg_t[:, r, :, e_i], scalar=SPv("sel4", selo + r, 1), in1=xh[:, :, col],
                                op0=ALU.mult, op1=ALU.add), [hg_b, sp_b, xh_b], [xh_b])
                xr = self.rot(ps, 1, [128, 16, 512], F32, "xd")
                hr = self.rot(ps, 1, [128, 16, 512], BF16, "hd")
                hh, hh_b = self.T(ps, [128, 16, 4], BF16, "hh")
                sqr = self.rot(ps, 2, [128, 512], F32, "sqd")
                tmpr = self.rot(ps, 2, [128, 512], F32, "tmpd")
                rstd, rstdb = self.T(ps, [128, 512], F32, "rstdd")
                pss, pssb = self.P(ps, name="pssd")
                wur = self.rot(ps, 6, [128, 16, 128], BF16, "wu")
                wdr = self.rot(ps, 2, [128, 44, 128], BF16, "wd")
                act_t, act_b = self.T(ps, [128, 44, 512], BF16, "act")
                cv_r = self.rot(ps, 2, [128, 512], F32, "cv")
                sl_r = self.rot(ps, 2, [128, 512], F32, "silu")
                hs_r = self.rot(ps, 2, [128, 4], F32, "hs")
                stg = self.rot(ps, 2, [128, 512], F32, "stgd")
                psa = self.rot(ps, 2, [128, 512], F32, "psa", psum=True)
                psb = self.rot(ps, 2, [128, 512], F32, "psb", psum=True)
                psh, psh_b = self.P(ps, name="psh")
                pso = self.rot(ps, 2, [128, 512], F32, "psod", psum=True)
                norm_tile(xh, xh_b, 4, hh, hh_b, l, 1, 1, sqr, pss, pssb, rstd, rstdb, tmpr)
                for ti in range(4):
                    kind = 0 if ti < 2 else 1
                    xt, xb = xr.next()
                    self.load(xt[:], xs[:, :, ti * 512:(ti + 1) * 512], xb)
                    h, hb = hr.next()
                    norm_tile(xt, xb, 512, h, hb, l, kind, 1, sqr, pss, pssb, rstd, rstdb, tmpr)
                    subs = [(0, 256), (256, 256)] if ti < 2 else [(0, 512)]
                    for j in range(44):
                        wa, wab = wur.next()
                        wb_, wbb = wur.next()
                        self.load(wa[:], wu[:, :, j * 128:(j + 1) * 128], wab)
                        self.load(wb_[:], wu[:, :, FF + j * 128: FF + (j + 1) * 128], wbb)
                        pa, pab = psa.next()
                        pb_, pbb = psb.next()
                        for kc in range(16):
                            op("pe", lambda e, pa=pa, wa=wa, h=h, kc=kc: e.matmul(pa[:], lhsT=wa[:, kc, :], rhs=h[:, kc, :],
                                                                                start=(kc == 0), stop=(kc == 15)), [wab, hb], [pab])
                        for kc in range(16):
                            op("pe", lambda e, pb_=pb_, wb_=wb_, h=h, kc=kc: e.matmul(pb_[:], lhsT=wb_[:, kc, :], rhs=h[:, kc, :],
                                                                                   start=(kc == 0), stop=(kc == 15)), [wbb, hb], [pbb])
                        c_, cb_ = cv_r.next()

                        def cw(tap, j=j):
                            return SPv("convw", (l * 3 + tap) * 44 + j, 1)
                        op("dve", lambda e, c_=c_, pa=pa, cw=cw, j=j: e.tensor_scalar(out=c_[:], in0=pa[:], scalar1=cw(1),
                                                                                     scalar2=SPv("convb", l * 44 + j, 1), op0=ALU.mult, op1=ALU.add),
                           [pab, sp_b], [cb_])
                        for (s0, sn) in subs:
                            op("dve", lambda e, c_=c_, pa=pa, cw=cw, s0=s0, sn=sn: e.scalar_tensor_tensor(
                                out=c_[:, s0 + 1:s0 + sn], in0=pa[:, s0:s0 + sn - 1], scalar=cw(0), in1=c_[:, s0 + 1:s0 + sn],
                                op0=ALU.mult, op1=ALU.add), [pab, sp_b, cb_], [cb_])
                            op("dve", lambda e, c_=c_, pa=pa, cw=cw, s0=s0, sn=sn: e.scalar_tensor_tensor(
                                out=c_[:, s0:s0 + sn - 1], in0=pa[:, s0 + 1:s0 + sn], scalar=cw(2), in1=c_[:, s0:s0 + sn - 1],
                                op0=ALU.mult, op1=ALU.add), [pab, sp_b, cb_], [cb_])
                        if ti >= 2:
                            for kc in range(16):
                                op("pe", lambda e, wa=wa, kc=kc: e.matmul(psh[:, 0:4], lhsT=wa[:, kc, :], rhs=hh[:, kc, :],
                                                                          start=(kc == 0), stop=(kc == 15)), [wab, hh_b], [psh_b])
                            hs, hsb = hs_r.next()
                            op("dve", lambda e, hs=hs: e.tensor_tensor(out=hs[:], in0=psh[:, 0:4], in1=SPv("edge4", 0, 4), op=ALU.mult),
                               [psh_b, sp_b], [hsb])
                            lc, rc = (0, 2) if ti == 2 else (1, 3)
                            op("dve", lambda e, hs=hs, c_=c_, cw=cw, lc=lc: e.scalar_tensor_tensor(
                                out=c_[:, 0:1], in0=hs[:, lc:lc + 1], scalar=cw(0), in1=c_[:, 0:1], op0=ALU.mult, op1=ALU.add),
                               [hsb, sp_b, cb_], [cb_])
                            op("dve", lambda e, hs=hs, c_=c_, cw=cw, rc=rc: e.scalar_tensor_tensor(
                                out=c_[:, 511:512], in0=hs[:, rc:rc + 1], scalar=cw(2), in1=c_[:, 511:512], op0=ALU.mult, op1=ALU.add),
                               [hsb, sp_b, cb_], [cb_])
                        s_, sb_ = sl_r.next()
                        op("act", lambda e, s_=s_, c_=c_: e.activation(out=s_[:], in_=c_[:], func=AF.Silu), [cb_], [sb_])
                        op("dve", lambda e, s_=s_, pb_=pb_, j=j: e.tensor_tensor(out=act_t[:, j, :], in0=s_[:], in1=pb_[:], op=ALU.mult),
                           [sb_, pbb], [act_b])
                    for ch in range(16):
                        w, wb2 = wdr.next()
                        self.load(w[:], wd[:, :, ch * 128:(ch + 1) * 128], wb2)
                        p_, pb2 = pso.next()
                        for kc in range(44):
                            op("pe", lambda e, p_=p_, w=w, kc=kc: e.matmul(p_[:], lhsT=w[:, kc, :], rhs=act_t[:, kc, :],
                                                                          start=(kc == 0), stop=(kc == 43)), [wb2, act_b], [pb2])
                        s_, sb_ = stg.next()
                        op("dve", lambda e, s_=s_, p_=p_, ch=ch, xt=xt, kind=kind: e.scalar_tensor_tensor(
                            out=s_[:], in0=p_[:], scalar=modt[:, l, kind, 80 + ch:81 + ch], in1=xt[:, ch, :], op0=ALU.mult, op1=ALU.add),
                           [pb2, modt_b, xb], [sb_])
                        self.store(xd[:, ch, ti * 512:(ti + 1) * 512], s_[:], sb_)
                fw.barrier()

        cols_r = colsT.ap()
        ymix_r = ymixT.ap()
        SEQS = [(0, 256, False, 0), (256, 256, False, 1), (512, 256, False, 2), (768, 256, False, 3), (1024, 1024, True, 0)]

        def att_prep(l):
            with ExitStack() as ps:
                rope_t, rope_b = self.T(ps, [128, 2048], F32, "rope")
                self.load(rope_t[:], rope_d.ap(), rope_b)
                xr = self.rot(ps, 2, [128, 512], F32, "ax")
                sqr = self.rot(ps, 2, [128, 512], F32, "asq")
                rs_r = self.rot(ps, 2, [128, 512], F32, "ars")
                xn_r = self.rot(ps, 2, [128, 512], F32, "axn")
                t1_r = self.rot(ps, 2, [128, 512], F32, "at1")
                qo_r = self.rot(ps, 2, [128, 512], BF16, "aqo")
                pss = self.rot(ps, 2, [128, 512], F32, "apss", psum=True)
                ppx = self.rot(ps, 2, [128, 512], F32, "appx", psum=True)
                for ci in range(5):
                    ch = 24 + ci
                    gname = "qng" if ci < 4 else "kng"
                    for ti in range(4):
                        x, xb = xr.next()
                        self.load(x[:], cols_r[ch * 128:(ch + 1) * 128, ti * 512:(ti + 1) * 512], xb)
                        s_, sb_ = sqr.next()
                        op("act", lambda e, s_=s_, x=x: e.activation(out=s_[:], in_=x[:], func=AF.Square), [xb], [sb_])
                        p_, pb = pss.next()
                        op("pe", lambda e, p_=p_, s_=s_: e.matmul(p_[:], lhsT=blk64_ap, rhs=s_[:], start=True, stop=True), [sb_, ct_b], [pb])
                        r_, rb = rs_r.next()
                        op("act", lambda e, r_=r_, p_=p_: e.activation(out=r_[:], in_=p_[:], func=AF.Sqrt, scale=1.0 / 64, bias=epsc[:]),
                           [pb, epsc_b], [rb])
                        op("dve", lambda e, r_=r_: e.reciprocal(out=r_[:], in_=r_[:]), [rb], [rb])
                        xn, xnb = xn_r.next()
                        op("dve", lambda e, xn=xn, x=x, r_=r_: e.scalar_tensor_tensor(out=xn[:], in0=x[:], scalar=SPv(gname, l, 1), in1=r_[:],
                                                                                     op0=ALU.mult, op1=ALU.mult), [xb, sp_b, rb], [xnb])
                        if ti >= 2:
                            p2, p2b = ppx.next()
                            op("pe", lambda e, p2=p2, xn=xn: e.matmul(p2[:], lhsT=CTv("perm", 0, 128), rhs=xn[:], start=True, stop=True),
                               [xnb, ct_b], [p2b])
                            so = (ti - 2) * 512
                            t1, t1b = t1_r.next()
                            op("dve", lambda e, t1=t1, p2=p2, so=so: e.tensor_tensor(out=t1[:], in0=p2[:], in1=rope_t[:, 1024 + so:1024 + so + 512],
                                                                                    op=ALU.mult), [p2b, rope_b], [t1b])
                            op("pool", lambda e, xn=xn, so=so: e.tensor_tensor(out=xn[:], in0=xn[:], in1=rope_t[:, so:so + 512], op=ALU.mult),
                               [xnb, rope_b], [xnb])
                            op("dve", lambda e, xn=xn, t1=t1: e.tensor_tensor(out=xn[:], in0=xn[:], in1=t1[:], op=ALU.add), [xnb, t1b], [xnb])
                        if ci < 4:
                            qo, qob = qo_r.next()
                            self.copy("pool", qo[:], xn[:], [xnb], [qob])
                            self.store(qnT.ap()[ci * 128:(ci + 1) * 128, ti * 512:(ti + 1) * 512], qo[:], qob)
                        else:
                            self.store(knT.ap()[:, ti * 512:(ti + 1) * 512], xn[:], xnb)
                fw.barrier()
            for sq_ in range(4):
                self.d2d(ck_out.ap()[l, sq_], knT.ap()[:, sq_ * 256:(sq_ + 1) * 256])
                self.d2d(cv_out.ap()[l, sq_], cols_r[29 * 128:30 * 128, sq_ * 256:(sq_ + 1) * 256])
            xb_ = xblob[l].ap()
            self.d2d(xb_[:, XO["kh"]:XO["kh"] + 128], knT.ap()[:, 1024:1152])
            self.d2d(xb_[:, XO["kh"] + 128:XO["kh"] + 256], knT.ap()[:, 1920:2048])
            self.d2d(xb_[:, XO["vh"]:XO["vh"] + 128], cols_r[29 * 128:30 * 128, 1024:1152])
            self.d2d(xb_[:, XO["vh"] + 128:XO["vh"] + 256], cols_r[29 * 128:30 * 128, 1920:2048])
            for gi in range(4):
                self.d2d(xb_[:, XO["pool"] + gi * 16: XO["pool"] + gi * 16 + 8], cols_r[gi * 128:(gi + 1) * 128, 1024:1032])
                self.d2d(xb_[:, XO["pool"] + gi * 16 + 8: XO["pool"] + gi * 16 + 16], cols_r[gi * 128:(gi + 1) * 128, 2040:2048])
            fw.barrier()

        def hg_tables(ps, l):
            hgl, hgl_b = self.T(ps, [128, 24], F32, "hgl")
            for d in range(2):
                for h in range(4):
                    o = (d * 4 + h) * 3
                    if l == 0:
                        op("dve", lambda e, o=o: e.memset(hgl[:, o:o + 1], 0.0), [], [hgl_b])
                    else:
                        r0 = SPv("hg_lbraw", (0 * 2 + d) * 4 + h, 1)
                        r1 = SPv("hg_lbraw", (1 * 2 + d) * 4 + h, 1)
                        op("dve", lambda e, o=o, r0=r0, r1=r1: e.tensor_tensor(out=hgl[:, o:o + 1], in0=r1, in1=r0, op=ALU.subtract),
                           [sp_b], [hgl_b])
                        op("act", lambda e, o=o: e.activation(out=hgl[:, o:o + 1], in_=hgl[:, o:o + 1], func=AF.Sigmoid), [hgl_b], [hgl_b])
                    op("dve", lambda e, o=o: e.tensor_scalar(out=hgl[:, o + 1:o + 2], in0=hgl[:, o:o + 1], scalar1=-1.0, scalar2=1.0,
                                                             op0=ALU.mult, op1=ALU.add), [hgl_b], [hgl_b])
                    op("dve", lambda e, o=o: e.tensor_scalar(out=hgl[:, o + 2:o + 3], in0=hgl[:, o + 1:o + 2], scalar1=-1.0, scalar2=None,
                                                             op0=ALU.mult), [hgl_b], [hgl_b])
            return hgl, hgl_b

        def hg_finalize(ps, l, h, oacc, oacc_b, g_t, g_b, T_, tok0, W):
            sq, sqb = W["sq"]
            rs, rsb = W["rs"]
            yo, yob = W["yo"]
            op("dve", lambda e: e.tensor_tensor(out=sq[:, 0:T_], in0=oacc[:, 0:T_], in1=oacc[:, 0:T_], op=ALU.mult), [oacc_b], [sqb])
            for t0 in range(0, T_, 512):
                tn = min(512, T_ - t0)
                p_, pb = W["pfin"].next()
                op("pe", lambda e, p_=p_, t0=t0, tn=tn: e.matmul(p_[:, 0:tn], lhsT=ones_ap, rhs=sq[:, t0:t0 + tn], start=True, stop=True),
                   [sqb, ct_b], [pb])
                op("act", lambda e, p_=p_, t0=t0, tn=tn: e.activation(out=rs[:, t0:t0 + tn], in_=p_[:, 0:tn], func=AF.Sqrt, scale=1.0 / 128,
                                                                      bias=epsc[:]), [pb, epsc_b], [rsb])
            op("dve", lambda e: e.reciprocal(out=rs[:, 0:T_], in_=rs[:, 0:T_]), [rsb], [rsb])
            op("dve", lambda e: e.tensor_tensor(out=rs[:, 0:T_], in0=rs[:, 0:T_], in1=oacc[:, 0:T_], op=ALU.mult), [rsb, oacc_b], [rsb])
            op("act", lambda e: e.activation(out=sq[:, 0:T_], in_=g_t[:, 0:T_], func=AF.Silu), [g_b], [sqb])
            op("dve", lambda e: e.scalar_tensor_tensor(out=yo[:, 0:T_], in0=rs[:, 0:T_], scalar=SPv("hg_ng", l, 1), in1=sq[:, 0:T_],
                                                       op0=ALU.mult, op1=ALU.mult), [rsb, sp_b, sqb], [yob])
            self.store(ymix_r[(4 + h) * 128:(5 + h) * 128, tok0:tok0 + T_], yo[:, 0:T_], yob)

        def hgrn_local(l):
            with ExitStack() as ps:
                hgl, hgl_b = hg_tables(ps, l)
                hm, hm_b = self.T(ps, [128, 2048], F32, "hmask")
                self.load(hm[:], hmask_d.ap(), hm_b)
                mk = lambda nm, dt=F32, n=1024: self.T(ps, [128, n], dt, nm)
                q_t, q_b = mk("q"); i_t, i_b = mk("i"); g_t, g_b = mk("g"); fx, fxb = mk("fx")
                sg, sgb = mk("sg"); lf, lfb = mk("lf"); kk, kkb = mk("kk"); cc, ccb = mk("cc")
                ex, exb = mk("ex"); c2, c2b = mk("c2")
                vb, vbb = mk("vb", BF16); qt, qtb = mk("qt", BF16); kt, ktb = mk("kt", BF16); kd, kdb = mk("kd", BF16)
                oacc, oacc_b = mk("oacc")
                Dv, Dvb = self.T(ps, [128, 64], F32, "Dv")
                Av, Avb = self.T(ps, [128, 1], F32, "Av")
                vtok, vtokb = self.T(ps, [128, 8, 128], BF16, "vtok")
                vexp, vexpb = self.T(ps, [128, 8, 1024], BF16, "vexp")
                Sf_r = self.rot(ps, 2, [128, 9, 128], F32, "Sf")
                Sb_r = self.rot(ps, 2, [128, 8, 128], BF16, "Sb")
                attm_r = self.rot(ps, 2, [128, 128], BF16, "attm")
                kdt_r = self.rot(ps, 2, [128, 128], BF16, "kdt")
                W = {"sq": mk("fsq"), "rs": mk("frs"), "yo": mk("fyo", BF16), "pfin": self.rot(ps, 1, [128, 512], F32, "pfin", psum=True)}
                ptr = self.rot(ps, 2, [128, 1024], BF16, "ptr", psum=True)
                pa_r = self.rot(ps, 2, [128, 512], F32, "pa", psum=True)
                po_r = self.rot(ps, 1, [128, 512], F32, "po", psum=True)
                pkv_r = self.rot(ps, 1, [128, 1024], F32, "pkv", psum=True)
                bm = CTv("bmask", 0, 8)
                import os as _os
                _cutn = int(_os.environ.get("HG_CUT", "0"))

                class _Cut(Exception):
                    pass

                def cut(k):
                    if _cutn == k:
                        raise _Cut()
                _seqs = SEQS[int(_os.environ.get("HG_S0", "0")):int(_os.environ.get("HG_S1", "5"))]
                cut(8)
                for (tok0, T_, is_s, sidx) in _seqs:
                    nb = T_ // 128
                    nch = T_ // 16
                    for h in range(int(_os.environ.get("HG_H", "4"))):
                        self.load(q_t[:, 0:T_], cols_r[(4 + h) * 128:(5 + h) * 128, tok0:tok0 + T_], q_b)
                        self.load(i_t[:, 0:T_], cols_r[(16 + h) * 128:(17 + h) * 128, tok0:tok0 + T_], i_b)
                        if not is_s:
                            self.load(g_t[:, 0:T_], cols_r[(20 + h) * 128:(21 + h) * 128, tok0:tok0 + T_], g_b)
                        self.copy("pool", vb[:, 0:T_], i_t[:, 0:T_], [i_b], [vbb])
                        cut(9)
                        for blk in range(nb):
                            pt, ptb = ptr.next()
                            op("pe", lambda e, pt=pt, blk=blk: e.transpose(pt[:, 0:128], vb[:, blk * 128:(blk + 1) * 128], identb[:]),
                               [vbb, identb_b], [ptb])
                            self.copy("dve", vtok[:, blk, :], pt[:, 0:128], [ptb], [vtokb])
                            op("dve", lambda e, pt=pt, blk=blk: e.tensor_tensor(
                                out=vexp[:, blk, :].rearrange("p (j v) -> p j v", v=128),
                                in0=pt[:, 0:128].unsqueeze(1).to_broadcast([128, 8, 128]),
                                in1=bm.unsqueeze(2).to_broadcast([128, 8, 128]), op=ALU.mult), [ptb, ct_b], [vexpb])
                        cut(1)
                        for d in range(2):
                            fch = (8 if d == 0 else 12) + h
                            self.load(fx[:, 0:T_], cols_r[fch * 128:(fch + 1) * 128, tok0:tok0 + T_], fxb)
                            o3 = (d * 4 + h) * 3
                            lbv, omlv, nomlv = hgl[:, o3:o3 + 1], hgl[:, o3 + 1:o3 + 2], hgl[:, o3 + 2:o3 + 3]
                            op("act", lambda e: e.activation(out=sg[:, 0:T_], in_=fx[:, 0:T_], func=AF.Sigmoid), [fxb], [sgb])
                            op("act", lambda e, omlv=omlv, lbv=lbv: e.activation(out=lf[:, 0:T_], in_=sg[:, 0:T_], func=AF.Ln, scale=omlv, bias=lbv),
                               [sgb, hgl_b], [lfb])
                            op("dve", lambda e, omlv=omlv, nomlv=nomlv: e.tensor_scalar(out=kk[:, 0:T_], in0=sg[:, 0:T_], scalar1=nomlv, scalar2=omlv,
                                                                                        op0=ALU.mult, op1=ALU.add), [sgb, hgl_b], [kkb])
                            if d == 0:
                                op("dve", lambda e: e.tensor_tensor_scan(out=cc[:, 0:T_], data0=hm[:, 0:T_], data1=lf[:, 0:T_], initial=0.0,
                                                                         op0=ALU.mult, op1=ALU.add), [hm_b, lfb], [ccb])
                            else:
                                op("dve", lambda e: e.tensor_tensor_scan(out=cc[:, 0:T_][:, ::-1],
                                                                         data0=hm[:, 1024:1024 + T_][:, ::-1], data1=lf[:, 0:T_][:, ::-1],
                                                                         initial=0.0, op0=ALU.mult, op1=ALU.add), [hm_b, lfb], [ccb])
                            c3 = cc[:, 0:T_].rearrange("p (n s) -> p n s", s=16)
                            totv = c3[:, :, 15] if d == 0 else c3[:, :, 0]
                            op("act", lambda e: e.activation(out=ex[:, 0:T_], in_=cc[:, 0:T_], func=AF.Exp), [ccb], [exb])
                            op("dve", lambda e: e.tensor_tensor(out=qt[:, 0:T_], in0=q_t[:, 0:T_], in1=ex[:, 0:T_], op=ALU.mult), [q_b, exb], [qtb])
                            op("act", lambda e: e.activation(out=ex[:, 0:T_], in_=cc[:, 0:T_], func=AF.Exp, scale=-1.0), [ccb, qtb], [exb])
                            op("dve", lambda e: e.tensor_tensor(out=kt[:, 0:T_], in0=kk[:, 0:T_], in1=ex[:, 0:T_], op=ALU.mult), [kkb, exb], [ktb])
                            op("dve", lambda e, totv=totv, c3=c3: e.tensor_tensor(
                                out=c2[:, 0:T_].rearrange("p (n s) -> p n s", s=16), in0=totv.unsqueeze(2).to_broadcast([128, nch, 16]),
                                in1=c3, op=ALU.subtract), [ccb], [c2b])
                            op("act", lambda e: e.activation(out=ex[:, 0:T_], in_=c2[:, 0:T_], func=AF.Exp), [c2b, ktb], [exb])
                            op("dve", lambda e: e.tensor_tensor(out=kd[:, 0:T_], in0=kk[:, 0:T_], in1=ex[:, 0:T_], op=ALU.mult), [kkb, exb], [kdb])
                            op("act", lambda e, totv=totv: e.activation(out=Dv[:, 0:nch], in_=totv, func=AF.Exp), [ccb], [Dvb])
                            if is_s:
                                op("dve", lambda e: e.tensor_reduce(out=Av[:], in_=lf[:, 0:T_], axis=mybir.AxisListType.X, op=ALU.add), [lfb], [Avb])
                                op("act", lambda e: e.activation(out=Av[:], in_=Av[:], func=AF.Exp), [Avb], [Avb])
                                self.store(xblob[l].ap()[:, XO["hgA"] + d * 4 + h: XO["hgA"] + d * 4 + h + 1], Av[:], Avb)
                            cut(2)
                            Sf, Sfb = Sf_r.next()
                            op("pool", lambda e, Sf=Sf: e.memset(Sf[:, 0, :], 0.0), [], [Sfb])
                            blks = range(nb) if d == 0 else range(nb - 1, -1, -1)
                            for blk in blks:
                                bs = slice(blk * 128, (blk + 1) * 128)
                                pa, pab = pa_r.next()
                                op("pe", lambda e, pa=pa, bs=bs: e.matmul(pa[:, 0:128], lhsT=kt[:, bs], rhs=qt[:, bs], start=True, stop=True),
                                   [ktb, qtb], [pab])
                                am, amb = attm_r.next()
                                op("dve", lambda e, am=am, pa=pa, d=d: e.tensor_tensor(out=am[:], in0=pa[:, 0:128], in1=CTv("cmask", d * 128, 128),
                                                                                      op=ALU.mult), [pab, ct_b], [amb])
                                cut(3)
                                pt, ptb = ptr.next()
                                op("pe", lambda e, pt=pt, bs=bs: e.transpose(pt[:, 0:128], kd[:, bs], identb[:]), [kdb, identb_b], [ptb])
                                kdt, kdtb = kdt_r.next()
                                self.copy("dve", kdt[:], pt[:, 0:128], [ptb], [kdtb])
                                pkv, pkvb = pkv_r.next()
                                for half in range(2):
                                    op("pe", lambda e, pkv=pkv, kdt=kdt, blk=blk, half=half: e.matmul(
                                        pkv[:, half * 512:(half + 1) * 512], lhsT=kdt[:], rhs=vexp[:, blk, half * 512:(half + 1) * 512],
                                        start=True, stop=True), [kdtb, vexpb], [pkvb])
                                cut(4)
                                for jj in range(8):
                                    j = jj if d == 0 else 7 - jj
                                    cg = blk * 8 + j
                                    op("dve", lambda e, Sf=Sf, jj=jj, j=j, cg=cg, pkv=pkv: e.scalar_tensor_tensor(
                                        out=Sf[:, jj + 1, :], in0=Sf[:, jj, :], scalar=Dv[:, cg:cg + 1], in1=pkv[:, j * 128:(j + 1) * 128],
                                        op0=ALU.mult, op1=ALU.add), [Sfb, Dvb, pkvb], [Sfb])
                                Sb, Sbb = Sb_r.next()
                                self.copy("pool", Sb[:], Sf[:, 0:8, :], [Sfb], [Sbb])
                                cut(5)
                                po, pob = po_r.next()
                                op("pe", lambda e, po=po, am=am, blk=blk: e.matmul(po[:, 0:128], lhsT=vtok[:, blk, :], rhs=am[:], start=True, stop=False),
                                   [vtokb, amb], [pob])
                                for jj in range(8):
                                    j = jj if d == 0 else 7 - jj
                                    t0 = blk * 128 + j * 16
                                    op("pe", lambda e, po=po, Sb=Sb, jj=jj, j=j, t0=t0: e.matmul(
                                        po[:, j * 16:(j + 1) * 16], lhsT=Sb[:, jj, :], rhs=qt[:, t0:t0 + 16], start=False, stop=(jj == 7)),
                                       [Sbb, qtb], [pob])
                                cut(6)
                                if d == 0:
                                    self.copy("act", oacc[:, bs], po[:, 0:128], [pob], [oacc_b])
                                else:
                                    op("dve", lambda e, po=po, bs=bs: e.tensor_tensor(out=oacc[:, bs], in0=oacc[:, bs], in1=po[:, 0:128], op=ALU.add),
                                       [pob, oacc_b], [oacc_b])
                                Sf2, Sf2b = Sf_r.next()
                                self.copy("pool", Sf2[:, 0, :], Sf[:, 8, :], [Sfb], [Sf2b])
                                Sf, Sfb = Sf2, Sf2b
                            if is_s:
                                o_ = XO["hgB"] + (d * 4 + h) * 128
                                self.store(xblob[l].ap()[:, o_:o_ + 128], Sf[:, 0, :], Sfb)
                            else:
                                self.store(hg_out.ap()[l, sidx, d, h], Sf[:, 0, :], Sfb)
                        cut(7)
                        if is_s:
                            self.store(oaccT.ap()[h * 128:(h + 1) * 128, :], oacc[:, 0:T_], oacc_b)
                        else:
                            hg_finalize(ps, l, h, oacc, oacc_b, g_t, g_b, T_, tok0, W)
                fw.barrier()

        def hgrn_post(l):
            xg = xgat[l].ap().rearrange("(r p) x -> p r x", p=128)
            with ExitStack() as ps:
                hgl, hgl_b = hg_tables(ps, l)
                mk = lambda nm, dt=F32, n=1024: self.T(ps, [128, n], dt, nm)
                q_t, q_b = mk("q"); g_t, g_b = mk("g"); fx, fxb = mk("fx"); sg, sgb = mk("sg"); lf, lfb = mk("lf")
                cc, ccb = mk("cc"); qg, qgb = mk("qg", BF16); oacc, oacc_b = mk("oacc")
                onesr, onesrb = mk("onesr")
                op("pool", lambda e: e.memset(onesr[:], 1.0), [], [onesrb])
                Bg, Bgb = self.T(ps, [128, 4, 128], F32, "Bg")
                Ag, Agb = self.T(ps, [128, 4, 8], F32, "Ag")
                self.load(Ag[:], xg[:, :, XO["hgA"]:XO["hgA"] + 8], Agb)
                S0, S0b = self.T(ps, [128, 128], F32, "S0")
                Sc, Scb = self.T(ps, [128, 128], F32, "Sc")
                Sm, Smb = self.T(ps, [128, 128], F32, "Sm")
                Smb16, Smb16b = self.T(ps, [128, 128], BF16, "Smb16")
                W = {"sq": mk("fsq"), "rs": mk("frs"), "yo": mk("fyo", BF16), "pfin": self.rot(ps, 1, [128, 512], F32, "pfin", psum=True)}
                pc_r = self.rot(ps, 2, [128, 512], F32, "pc", psum=True)
                T_ = 1024
                tok0 = 1024
                for h in range(4):
                    self.load(q_t[:], cols_r[(4 + h) * 128:(5 + h) * 128, tok0:tok0 + T_], q_b)
                    self.load(g_t[:], cols_r[(20 + h) * 128:(21 + h) * 128, tok0:tok0 + T_], g_b)
                    self.load(oacc[:], oaccT.ap()[h * 128:(h + 1) * 128, :], oacc_b)
                    for d in range(2):
                        o_ = XO["hgB"] + (d * 4 + h) * 128
                        self.load(Bg[:], xg[:, :, o_:o_ + 128], Bgb)
                        self.load(S0[:], hg0_d.ap()[l, d, h], S0b)
                        order = [0, 1, 2, 3] if d == 0 else [3, 2, 1, 0]
                        self.copy("dve", Sc[:], S0[:], [S0b], [Scb])
                        for n_, j in enumerate(order):
                            selv = SPv("sel4", 8 + j, 1)
                            if n_ == 0:
                                op("dve", lambda e, selv=selv: e.tensor_scalar(out=Sm[:], in0=Sc[:], scalar1=selv, scalar2=None, op0=ALU.mult),
                                   [Scb, sp_b], [Smb])
                            else:
                                op("dve", lambda e, selv=selv: e.scalar_tensor_tensor(out=Sm[:], in0=Sc[:], scalar=selv, in1=Sm[:], op0=ALU.mult,
                                                                                      op1=ALU.add), [Scb, sp_b, Smb], [Smb])
                            if n_ < 3:
                                op("dve", lambda e, j=j, d=d, h=h: e.scalar_tensor_tensor(
                                    out=Sc[:], in0=Sc[:], scalar=Ag[:, j, d * 4 + h:d * 4 + h + 1], in1=Bg[:, j, :], op0=ALU.mult, op1=ALU.add),
                                   [Scb, Agb, Bgb], [Scb])
                        self.copy("act", Smb16[:], Sm[:], [Smb], [Smb16b])
                        fch = (8 if d == 0 else 12) + h
                        self.load(fx[:], cols_r[fch * 128:(fch + 1) * 128, tok0:tok0 + T_], fxb)
                        o3 = (d * 4 + h) * 3
                        lbv, omlv = hgl[:, o3:o3 + 1], hgl[:, o3 + 1:o3 + 2]
                        op("act", lambda e: e.activation(out=sg[:], in_=fx[:], func=AF.Sigmoid), [fxb], [sgb])
                        op("act", lambda e, omlv=omlv, lbv=lbv: e.activation(out=lf[:], in_=sg[:], func=AF.Ln, scale=omlv, bias=lbv),
                           [sgb, hgl_b], [lfb])
                        if d == 0:
                            op("dve", lambda e: e.tensor_tensor_scan(out=cc[:], data0=onesr[:], data1=lf[:], initial=0.0, op0=ALU.mult, op1=ALU.add),
                               [onesrb, lfb], [ccb])
                        else:
                            op("dve", lambda e: e.tensor_tensor_scan(out=cc[:, ::-1], data0=onesr[:], data1=lf[:, ::-1], initial=0.0,
                                                                     op0=ALU.mult, op1=ALU.add), [onesrb, lfb], [ccb])
                        op("act", lambda e: e.activation(out=cc[:], in_=cc[:], func=AF.Exp), [ccb], [ccb])
                        op("dve", lambda e: e.tensor_tensor(out=qg[:], in0=q_t[:], in1=cc[:], op=ALU.mult), [q_b, ccb], [qgb])
                        for t0 in (0, 512):
                            p_, pb = pc_r.next()
                            op("pe", lambda e, p_=p_, t0=t0: e.matmul(p_[:], lhsT=Smb16[:], rhs=qg[:, t0:t0 + 512], start=True, stop=True),
                               [Smb16b, qgb], [pb])
                            op("dve", lambda e, p_=p_, t0=t0: e.tensor_tensor(out=oacc[:, t0:t0 + 512], in0=oacc[:, t0:t0 + 512], in1=p_[:], op=ALU.add),
                               [pb, oacc_b], [oacc_b])
                    hg_finalize(ps, l, h, oacc, oacc_b, g_t, g_b, T_, tok0, W)
                fw.barrier()

        def pool_mixer(l):
            xg = xgat[l].ap().rearrange("(r p) x -> p r x", p=128)
            with ExitStack() as ps:
                pic, picb = self.T(ps, [128, 5120], F32, "pic")
                self.load(pic[:], pic_d.ap(), picb)
                pw, pwb = self.T(ps, [128, 4, 128], F32, "pw")
                self.load(pw[:], pool_w_d.ap()[l].rearrange("g c e -> c g e"), pwb)
                pwh, pwhb = self.T(ps, [128, 4, 128], BF16, "pwh")
                self.copy("dve", pwh[:], pw[:], [pwb], [pwhb])
                hal, halb = self.T(ps, [128, 4, 64], F32, "hal")
                self.load(hal[:], xg[:, :, XO["pool"]:XO["pool"] + 64], halb)
                ur = self.rot(ps, 2, [128, 1040], F32, "pu")
                ar = self.rot(ps, 2, [128, 1040], F32, "pa_")
                br = self.rot(ps, 2, [128, 1040], F32, "pb_")
                pdr = self.rot(ps, 2, [128, 1024], BF16, "ppd")
                yor = self.rot(ps, 2, [128, 1024], BF16, "pyo")
                pp = self.rot(ps, 2, [128, 512], F32, "ppp", psum=True)
                for (tok0, T_, is_s, sidx) in SEQS:
                    for gi in range(4):
                        w = 2 << gi
                        u, ub = ur.next()
                        op("pool", lambda e, u=u: e.memset(u[:, 0:8], 0.0), [], [ub])
                        op("pool", lambda e, u=u, T_=T_: e.memset(u[:, 8 + T_:16 + T_], 0.0), [], [ub])
                        self.load(u[:, 8:8 + T_], cols_r[gi * 128:(gi + 1) * 128, tok0:tok0 + T_], ub)
                        if is_s:
                            for r in range(4):
                                op("dve", lambda e, u=u, r=r, gi=gi: e.scalar_tensor_tensor(
                                    out=u[:, 0:8], in0=hal[:, r, gi * 16 + 8:gi * 16 + 16], scalar=SPv("sel4", r, 1), in1=u[:, 0:8],
                                    op0=ALU.mult, op1=ALU.add), [halb, sp_b, ub], [ub])
                                op("dve", lambda e, u=u, r=r, gi=gi, T_=T_: e.scalar_tensor_tensor(
                                    out=u[:, 8 + T_:16 + T_], in0=hal[:, r, gi * 16:gi * 16 + 8], scalar=SPv("sel4", 4 + r, 1),
                                    in1=u[:, 8 + T_:16 + T_], op0=ALU.mult, op1=ALU.add), [halb, sp_b, ub], [ub])
                        n = T_ + 16
                        src, srcb = u, ub
                        m = 1
                        while m < w:
                            dst, dstb = (ar if (m in (1, 4)) else br).next()
                            n2 = n - m
                            op("dve" if m in (1, 4) else "pool", lambda e, dst=dst, src=src, n2=n2, m=m: e.tensor_tensor(
                                out=dst[:, 0:n2], in0=src[:, 0:n2], in1=src[:, m:m + n2], op=ALU.add), [srcb], [dstb])
                            src, srcb, n, m = dst, dstb, n2, m * 2
                        o0 = 8 - w // 2
                        pico = (0 if not is_s else 1024) + gi * T_
                        tmp, tmpb = (ar if src is not ar.items[0][0] and src is not ar.items[1][0] else br).next()
                        op("dve", lambda e, tmp=tmp, src=src, o0=o0, pico=pico, T_=T_: e.tensor_tensor(
                            out=tmp[:, 0:T_], in0=src[:, o0:o0 + T_], in1=pic[:, pico:pico + T_], op=ALU.mult), [srcb, picb], [tmpb])
                        pd, pdb = pdr.next()
                        op("dve", lambda e, pd=pd, tmp=tmp, u=u, T_=T_: e.tensor_tensor(out=pd[:, 0:T_], in0=tmp[:, 0:T_], in1=u[:, 8:8 + T_],
                                                                                       op=ALU.subtract), [tmpb, ub], [pdb])
                        yo, yob = yor.next()
                        for t0 in range(0, T_, 512):
                            tn = min(512, T_ - t0)
                            p_, pb = pp.next()
                            op("pe", lambda e, p_=p_, pd=pd, gi=gi, t0=t0, tn=tn: e.matmul(p_[:, 0:tn], lhsT=pwh[:, gi, :], rhs=pd[:, t0:t0 + tn],
                                                                                          start=True, stop=True), [pwhb, pdb], [pb])
                            op("act", lambda e, p_=p_, yo=yo, t0=t0, tn=tn, gi=gi: e.activation(
                                out=yo[:, t0:t0 + tn], in_=p_[:, 0:tn], func=AF.Copy, scale=SPv("pool_scale", l * 4 + gi, 1)), [pb, sp_b], [yob])
                        self.store(ymix_r[gi * 128:(gi + 1) * 128, tok0:tok0 + T_], yo[:, 0:T_], yob)
                fw.barrier()

        def attention(l):
            xg = xgat[l].ap().rearrange("(r p) x -> p r x", p=128)
            with ExitStack() as ps:
                esk, eskb = self.T(ps, [128, 4], F32, "esk")
                op("act", lambda e: e.activation(out=esk[:], in_=SPv("sink", l * 4, 4), func=AF.Exp), [sp_b], [eskb])
                onesb, onesbb = self.T(ps, [128, 2, 128], BF16, "onesb")
                op("pool", lambda e: e.memset(onesb[:], 0.0), [], [onesbb])
                op("pool", lambda e: e.memset(onesb[:, 0, 0:64], 1.0), [], [onesbb])
                op("pool", lambda e: e.memset(onesb[:, 1, 64:128], 1.0), [], [onesbb])
                am_t, am_b = self.T(ps, [128, 4, 128], BF16, "amask")
                self.copy("dve", am_t[:].rearrange("p a b -> p (a b)"), CTv("amask", 0, 512), [ct_b], [am_b])
                NKB = 14
                kf, kfb = self.T(ps, [128, 1280], F32, "kf")
                k2, k2b = self.T(ps, [128, 2, 1536], BF16, "k2")
                vf, vfb = self.T(ps, [128, 1280], F32, "vf")
                vb_, vbb = self.T(ps, [128, 1280], BF16, "vb")
                cvf, cvfb = self.T(ps, [128, 2, 128], F32, "cvf")
                ckf, ckfb = self.T(ps, [128, 2, 256], F32, "ckf")
                VA, VAb = self.T(ps, [128, 12, 2, 2, 128], BF16, "VA")
                op("pool", lambda e: e.memset(VA[:].rearrange("p a b c d -> p (a b c d)"), 0.0), [], [VAb])
                hk, hkb = self.T(ps, [128, 4, 512], F32, "hk")
                qr = self.rot(ps, 2, [128, 1024], BF16, "aq")
                pt_r = self.rot(ps, 3, [128, 512], BF16, "apt")
                yo_r = self.rot(ps, 2, [128, 1024], BF16, "ayo")
                dn_r = self.rot(ps, 2, [128, 512], F32, "adn")
                ptr = self.rot(ps, 2, [128, 1024], BF16, "atr", psum=True)
                ps_r = self.rot(ps, 3, [128, 512], F32, "aps", psum=True)
                po_r = self.rot(ps, 1, [128, 512], F32, "apo", psum=True)
                pd_r = self.rot(ps, 1, [128, 512], F32, "apd", psum=True)

                def build_kv(nkeys, kb0):
                    self.copy("pool", vb_[:, 0:nkeys], vf[:, 0:nkeys], [vfb], [vbb])
                    for kb in range(nkeys // 128):
                        pt, ptb = ptr.next()
                        op("pe", lambda e, pt=pt, kb=kb: e.transpose(pt[:, 0:128], vb_[:, kb * 128:(kb + 1) * 128], identb[:]), [vbb, identb_b], [ptb])
                        for kvh in range(2):
                            self.copy("dve", VA[:, kb0 + kb, kvh, 0, 0:64], pt[:, kvh * 64:(kvh + 1) * 64], [ptb], [VAb])
                            self.copy("dve", VA[:, kb0 + kb, kvh, 1, 64:128], pt[:, kvh * 64:(kvh + 1) * 64], [ptb], [VAb])

                def load_k2(src_of_rows, nkeys, k0):
                    for kvh in range(2):
                        for half in range(2):
                            self.load(kf[half * 64:(half + 1) * 64, 0:nkeys], src_of_rows(kvh), kfb)
                        self.copy("dve", k2[:, kvh, k0:k0 + nkeys], kf[:, 0:nkeys], [kfb], [k2b])

                def attend(tok0, nq_blocks, keyblocks_for, ci_list=range(4)):
                    for ci in ci_list:
                        kvh = ci // 2
                        q, qb_ = qr.next()
                        nq = nq_blocks * 128
                        self.load(q[:, 0:nq], qnT.ap()[ci * 128:(ci + 1) * 128, tok0:tok0 + nq], qb_)
                        yo, yob = yo_r.next()
                        for qb in range(nq_blocks):
                            kbl = keyblocks_for(qb)
                            po, pob = po_r.next()
                            pd, pdb = pd_r.next()
                            n_mm = len(kbl) * 2
                            i_mm = 0
                            for (kb, mi) in kbl:
                                for hh in range(2):
                                    rs = slice(hh * 64, (hh + 1) * 64)
                                    p_, pb = ps_r.next()
                                    op("pe", lambda e, p_=p_, kb=kb, rs=rs, q=q, qb=qb, kvh=kvh: e.matmul(
                                        p_[:, 0:128], lhsT=k2[rs, kvh, kb * 128:(kb + 1) * 128], rhs=q[rs, qb * 128:(qb + 1) * 128],
                                        start=True, stop=True), [k2b, qb_], [pb])
                                    pt, ptb = pt_r.next()
                                    op("act", lambda e, pt=pt, p_=p_: e.activation(out=pt[:, 0:128], in_=p_[:, 0:128], func=AF.Exp, scale=0.125),
                                       [pb], [ptb])
                                    if mi is not None:
                                        op("dve", lambda e, pt=pt, mi=mi: e.tensor_tensor(out=pt[:, 0:128], in0=pt[:, 0:128], in1=am_t[:, mi, :],
                                                                                         op=ALU.mult), [ptb, am_b], [ptb])
                                    st, sp_ = (i_mm == 0), (i_mm == n_mm - 1)
                                    op("pe", lambda e, po=po, pt=pt, kb=kb, kvh=kvh, hh=hh, st=st, sp_=sp_: e.matmul(
                                        po[:, 0:128], lhsT=VA[:, kb, kvh, hh, :], rhs=pt[:, 0:128], start=st, stop=sp_), [VAb, ptb], [pob])
                                    op("pe", lambda e, pd=pd, pt=pt, hh=hh, st=st, sp_=sp_: e.matmul(
                                        pd[:, 0:128], lhsT=onesb[:, hh, :], rhs=pt[:, 0:128], start=st, stop=sp_), [onesbb, ptb], [pdb])
                                    i_mm += 1
                            dn, dnb = dn_r.next()
                            op("dve", lambda e, dn=dn, pd=pd, ci=ci: e.tensor_scalar(out=dn[:, 0:128], in0=pd[:, 0:128], scalar1=esk[:, ci:ci + 1],
                                                                                    scalar2=None, op0=ALU.add), [pdb, eskb], [dnb])
                            op("dve", lambda e, dn=dn: e.reciprocal(out=dn[:, 0:128], in_=dn[:, 0:128]), [dnb], [dnb])
                            op("dve", lambda e, dn=dn, po=po, yo=yo, qb=qb: e.tensor_tensor(out=yo[:, qb * 128:(qb + 1) * 128], in0=po[:, 0:128],
                                                                                            in1=dn[:, 0:128], op=ALU.mult), [pob, dnb], [yob])
                        self.store(ymix_r[(8 + ci) * 128:(9 + ci) * 128, tok0:tok0 + nq], yo[:, 0:nq], yob)

                for sq_ in range(4):
                    tok0 = sq_ * 256
                    load_k2(lambda kvh, tok0=tok0: knT.ap()[kvh * 64:(kvh + 1) * 64, tok0:tok0 + 256], 256, 0)
                    self.load(vf[:, 0:256], cols_r[29 * 128:30 * 128, tok0:tok0 + 256], vfb)
                    build_kv(256, 0)
                    attend(tok0, 2, lambda qb: [(0, None), (1, None)])
                self.load(hk[:], xg[:, :, XO["kh"]:XO["kh"] + 512], hkb)
                self.load(vf[:, 128:1152], cols_r[29 * 128:30 * 128, 1024:2048], vfb)
                for (dst0, e0, selo) in ((0, 384, 0), (1152, 256, 4)):
                    for r in range(4):
                        if r == 0:
                            op("dve", lambda e, dst0=dst0, e0=e0, selo=selo, r=r: e.tensor_scalar(
                                out=vf[:, dst0:dst0 + 128], in0=hk[:, r, e0:e0 + 128], scalar1=SPv("sel4", selo + r, 1), scalar2=None, op0=ALU.mult),
                               [hkb, sp_b], [vfb])
                        else:
                            op("dve", lambda e, dst0=dst0, e0=e0, selo=selo, r=r: e.scalar_tensor_tensor(
                                out=vf[:, dst0:dst0 + 128], in0=hk[:, r, e0:e0 + 128], scalar=SPv("sel4", selo + r, 1), in1=vf[:, dst0:dst0 + 128],
                                op0=ALU.mult, op1=ALU.add), [hkb, sp_b, vfb], [vfb])
                build_kv(1280, 0)
                self.load(cvf[:], cv_d.ap()[l].rearrange("(b k) f -> k b f", k=128), cvfb)
                for kb in range(2):
                    for kvh in range(2):
                        self.copy("act", VA[:, 10 + kb, kvh, 0, 0:64], cvf[:, kb, kvh * 64:(kvh + 1) * 64], [cvfb], [VAb])
                        self.copy("dve", VA[:, 10 + kb, kvh, 1, 64:128], cvf[:, kb, kvh * 64:(kvh + 1) * 64], [cvfb], [VAb])
                hsel, hselb = self.T(ps, [128, 256], F32, "hsel")
                for (dst0, e0, selo) in ((0, 128, 0), (128, 0, 4)):
                    for r in range(4):
                        if r == 0:
                            op("dve", lambda e, dst0=dst0, e0=e0, selo=selo, r=r: e.tensor_scalar(
                                out=hsel[:, dst0:dst0 + 128], in0=hk[:, r, e0:e0 + 128], scalar1=SPv("sel4", selo + r, 1), scalar2=None, op0=ALU.mult),
                               [hkb, sp_b], [hselb])
                        else:
                            op("dve", lambda e, dst0=dst0, e0=e0, selo=selo, r=r: e.scalar_tensor_tensor(
                                out=hsel[:, dst0:dst0 + 128], in0=hk[:, r, e0:e0 + 128], scalar=SPv("sel4", selo + r, 1),
                                in1=hsel[:, dst0:dst0 + 128], op0=ALU.mult, op1=ALU.add), [hkb, sp_b, hselb], [hselb])
                self.store(khalo.ap(), hsel[:], hselb)
                fw.barrier()
                for kvh in range(2):
                    for half in range(2):
                        rs = slice(half * 64, (half + 1) * 64)
                        self.load(kf[rs, 0:128], khalo.ap()[kvh * 64:(kvh + 1) * 64, 0:128], kfb)
                        self.load(kf[rs, 1152:1280], khalo.ap()[kvh * 64:(kvh + 1) * 64, 128:256], kfb)
                        self.load(kf[rs, 128:1152], knT.ap()[kvh * 64:(kvh + 1) * 64, 1024:2048], kfb)
                    self.copy("dve", k2[:, kvh, 0:1280], kf[:, 0:1280], [kfb], [k2b])
                    self.load(ckf[:, kvh, :], ckT_d.ap()[l, kvh], ckfb)
                    self.copy("dve", k2[:, kvh, 1280:1536], ckf[:, kvh, :], [ckfb], [k2b])

                def kbl_sample(qb):
                    prev_m = 2 if qb == 0 else 0
                    next_m = 3 if qb == 7 else 1
                    return [(qb, prev_m), (qb + 1, None), (qb + 2, next_m), (10, None), (11, None)]
                attend(1024, 8, kbl_sample)
                fw.barrier()

        def s5_phase(l, mode):
            xg = xgat[l].ap().rearrange("(r p) x -> p r x", p=128)
            L = 128
            with ExitStack() as ps:
                small = lambda nm, n=32: self.T(ps, [128, n], F32, nm)
                are, areb = small("are"); aim, aimb = small("aim"); dt_, dtb = small("dt")
                op("act", lambda e: e.activation(out=dt_[:], in_=SPv("s5ldt", l * 32, 32), func=AF.Exp), [sp_b], [dtb])
                er, erb = small("er"); th, thb = small("th"); rm, rmb = small("rm")
                op("dve", lambda e: e.tensor_tensor(out=er[:], in0=SPv("s5are", l * 32, 32), in1=dt_[:], op=ALU.mult), [sp_b, dtb], [erb])
                op("dve", lambda e: e.tensor_tensor(out=th[:], in0=SPv("s5aim", l * 32, 32), in1=dt_[:], op=ALU.mult), [sp_b, dtb], [thb])
                op("act", lambda e: e.activation(out=rm[:], in_=er[:], func=AF.Exp), [erb], [rmb])

                def sincos(ang, angb, n, snm, tps, pre=None):
                    outs = []
                    if pre is None:
                        pre = [self.T(ps, [128, n], F32, snm + "a") for _ in range(2)]
                    k, kb = self.T(tps, [128, n], F32, snm + "k")
                    ki, kib = self.T(tps, [128, n], I32, snm + "ki")
                    for shift, (a, ab) in zip((0.0, PI / 2), pre):
                        op("dve", lambda e, a=a: e.tensor_scalar(out=a[:], in0=ang[:], scalar1=shift, scalar2=None, op0=ALU.add), [angb], [ab])
                        op("dve", lambda e, a=a, k=k: e.tensor_scalar(out=k[:], in0=a[:], scalar1=1.0 / (2 * PI), scalar2=None, op0=ALU.mult), [ab], [kb])
                        op("dve", lambda e, k=k, ki=ki: e.tensor_copy(out=ki[:], in_=k[:]), [kb], [kib])
                        op("dve", lambda e, k=k, ki=ki: e.tensor_copy(out=k[:], in_=ki[:]), [kib], [kb])
                        op("dve", lambda e, a=a, k=k: e.scalar_tensor_tensor(out=a[:], in0=k[:], scalar=-2 * PI, in1=a[:], op0=ALU.mult, op1=ALU.add),
                           [kb, ab], [ab])
                        op("dve", lambda e, a=a, k=k: e.tensor_scalar(out=k[:], in0=a[:], scalar1=-PI, scalar2=2 * PI, op0=ALU.is_lt, op1=ALU.mult),
                           [ab], [kb])
                        op("dve", lambda e, a=a, k=k: e.tensor_tensor(out=a[:], in0=a[:], in1=k[:], op=ALU.add), [ab, kb], [ab])
                        op("dve", lambda e, a=a, k=k: e.tensor_scalar(out=k[:], in0=a[:], scalar1=PI, scalar2=-2 * PI, op0=ALU.is_gt, op1=ALU.mult),
                           [ab], [kb])
                        op("dve", lambda e, a=a, k=k: e.tensor_tensor(out=a[:], in0=a[:], in1=k[:], op=ALU.add), [ab, kb], [ab])
                        op("dve", lambda e, a=a: e.tensor_scalar(out=a[:], in0=a[:], scalar1=-PI, scalar2=PI, op0=ALU.max, op1=ALU.min), [ab], [ab])
                        op("act", lambda e, a=a: e.activation(out=a[:], in_=a[:], func=AF.Sin), [ab], [ab])
                        outs.append((a, ab))
                    return outs[0], outs[1]

                (sth, sthb), (cth, cthb) = sincos(th, thb, 32, "t0", ps)
                nr, nrb = small("nr"); ni, nib = small("ni"); den, denb = small("den"); t_, tb = small("t_")
                fr, frb = small("fr"); fi, fib = small("fi"); nfr, nfrb = small("nfr")
                op("dve", lambda e: e.tensor_tensor(out=nr[:], in0=rm[:], in1=cth[:], op=ALU.mult), [rmb, cthb], [nrb])
                op("dve", lambda e: e.tensor_scalar(out=nr[:], in0=nr[:], scalar1=-1.0, scalar2=None, op0=ALU.add), [nrb], [nrb])
                op("dve", lambda e: e.tensor_tensor(out=ni[:], in0=rm[:], in1=sth[:], op=ALU.mult), [rmb, sthb], [nib])
                A_re, A_im = SPv("s5are", l * 32, 32), SPv("s5aim", l * 32, 32)
                op("dve", lambda e: e.tensor_tensor(out=den[:], in0=A_re, in1=A_re, op=ALU.mult), [sp_b], [denb])
                op("dve", lambda e: e.tensor_tensor(out=t_[:], in0=A_im, in1=A_im, op=ALU.mult), [sp_b], [tb])
                op("dve", lambda e: e.tensor_tensor(out=den[:], in0=den[:], in1=t_[:], op=ALU.add), [denb, tb], [denb])
                op("dve", lambda e: e.reciprocal(out=den[:], in_=den[:]), [denb], [denb])
                op("dve", lambda e: e.tensor_tensor(out=fr[:], in0=nr[:], in1=A_re, op=ALU.mult), [nrb, sp_b], [frb])
                op("dve", lambda e: e.tensor_tensor(out=t_[:], in0=ni[:], in1=A_im, op=ALU.mult), [nib, sp_b], [tb])
                op("dve", lambda e: e.tensor_tensor(out=fr[:], in0=fr[:], in1=t_[:], op=ALU.add), [frb, tb], [frb])
                op("dve", lambda e: e.tensor_tensor(out=fr[:], in0=fr[:], in1=den[:], op=ALU.mult), [frb, denb], [frb])
                op("dve", lambda e: e.tensor_tensor(out=fi[:], in0=ni[:], in1=A_re, op=ALU.mult), [nib, sp_b], [fib])
                op("dve", lambda e: e.tensor_tensor(out=t_[:], in0=nr[:], in1=A_im, op=ALU.mult), [nrb, sp_b], [tb])
                op("dve", lambda e: e.tensor_tensor(out=fi[:], in0=fi[:], in1=t_[:], op=ALU.subtract), [fib, tb], [fib])
                op("dve", lambda e: e.tensor_tensor(out=fi[:], in0=fi[:], in1=den[:], op=ALU.mult), [fib, denb], [fib])
                op("dve", lambda e: e.tensor_scalar(out=nfr[:], in0=fr[:], scalar1=-1.0, scalar2=None, op0=ALU.mult), [frb], [nfrb])
                pre_sc = [self.T(ps, [128, 32 * L], F32, "sctab") for _ in range(2)]
                with ExitStack() as tps:
                    ang, angb = self.T(tps, [128, 32 * L], F32, "ang")
                    for i in range(32):
                        op("dve", lambda e, i=i: e.tensor_scalar(out=ang[:, i * L:(i + 1) * L], in0=CTv("iota1", 0, L), scalar1=th[:, i:i + 1], scalar2=None,
                                                                 op0=ALU.mult), [ct_b, thb], [angb])
                    (SN, SNb), (CS, CSb) = sincos(ang, angb, 32 * L, "tb", tps, pre_sc)
                    fw.barrier()
                T1, T1b = self.T(ps, [128, 32 * L], F32, "T1")
                T2, T2b = self.T(ps, [128, 32 * L], F32, "T2")
                for i in range(32):
                    sl = slice(i * L, (i + 1) * L)
                    op("dve", lambda e, i=i, sl=sl: e.tensor_scalar(out=T1[:, sl], in0=CS[:, sl], scalar1=fr[:, i:i + 1], scalar2=None, op0=ALU.mult),
                       [CSb, frb], [T1b])
                    op("dve", lambda e, i=i, sl=sl: e.scalar_tensor_tensor(out=T1[:, sl], in0=SN[:, sl], scalar=fi[:, i:i + 1], in1=T1[:, sl],
                                                                          op0=ALU.mult, op1=ALU.add), [SNb, fib, T1b], [T1b])
                    op("dve", lambda e, i=i, sl=sl: e.tensor_scalar(out=T2[:, sl], in0=CS[:, sl], scalar1=fi[:, i:i + 1], scalar2=None, op0=ALU.mult),
                       [CSb, fib], [T2b])
                    op("dve", lambda e, i=i, sl=sl: e.scalar_tensor_tensor(out=T2[:, sl], in0=SN[:, sl], scalar=nfr[:, i:i + 1], in1=T2[:, sl],
                                                                          op0=ALU.mult, op1=ALU.add), [SNb, nfrb, T2b], [T2b])
                Bp, Bpb = self.T(ps, [128, 2, 32, 128], BF16, "Bp")
                Cp, Cpb = self.T(ps, [128, 2, 32, 128], BF16, "Cp")
                stgr = self.rot(ps, 2, [128, 16, 128], F32, "s5stg")
                for which in range(2):
                    for d in range(2):
                        for ri in range(2):
                            st_, stb = stgr.next()
                            op("pool", lambda e, st_=st_: e.memset(st_[:].rearrange("p a b -> p (a b)"), 0.0), [], [stb])
                            for r4 in range(4):
                                for gl in range(2):
                                    if which == 0:
                                        src_ = s5b_d.ap()[l, d, ri].rearrange("(a q) c n -> q c a n", q=8)[2 * r4 + gl]
                                        dst_ = st_[r4 * 32 + gl * 16: r4 * 32 + gl * 16 + 16, :, gl * 64:(gl + 1) * 64]
                                    else:
                                        src_ = s5c_d.ap()[l, d, ri].rearrange("(a q) n c -> q n a c", q=8)[2 * r4 + gl]
                                        dst_ = st_[gl * 64:(gl + 1) * 64, :, r4 * 32 + gl * 16: r4 * 32 + gl * 16 + 16]
                                    dst_ = dst_.rearrange("p (a r) n -> p a r n", r=4)[:, :, r4, :]
                                    self.load(dst_, src_, stb)
                            tgt = Bp if which == 0 else Cp
                            tgtb = Bpb if which == 0 else Cpb
                            self.copy("dve", tgt[:, ri, d * 16:(d + 1) * 16, :], st_[:], [stb], [tgtb])
                glu_h, glu_hb = self.T(ps, [128, 4, 1024], BF16, "gluh")
                if mode == "main":
                    for kc in range(4):
                        st_, stb = stgr.next()
                        stv = st_[:].rearrange("p a b -> p (a b)")[:, 0:1024]
                        self.load(stv, glu_d.ap()[l, kc * 128:(kc + 1) * 128, :], stb)
                        self.copy("pool", glu_h[:, kc, :], stv, [stb], [glu_hb])
                hin, hinb = self.T(ps, [128, 32, 2], F32, "hin")
                uf_r = self.rot(ps, 2, [128, 4, L], F32, "uf")
                ub_r = self.rot(ps, 2, [128, 4, L], BF16, "ub")
                wk = {nm: self.rot(ps, 3, [128, L], F32, "w" + nm) for nm in ("t1", "t2", "wr", "wi", "gr", "gi")}
                wkh = {nm: self.rot(ps, 2, [128, L], BF16, "w" + nm) for nm in ("hr", "hi")}
                yacc, yaccb = self.T(ps, [128, 4, 1024], F32, "yacc")
                ygl, yglb = self.T(ps, [128, 4, 1024], BF16, "ygl")
                zo_r = self.rot(ps, 2, [128, 512], BF16, "zo")
                sgt_r = self.rot(ps, 2, [128, 512], F32, "sgt")
                px_r = self.rot(ps, 2, [128, 512], F32, "px", psum=True)
                py_r = self.rot(ps, 2, [128, 512], F32, "py", psum=True)
                pz_r = self.rot(ps, 2, [128, 512], F32, "pz", psum=True)
                pg_r = self.rot(ps, 2, [128, 512], F32, "pg", psum=True)

                def run_seq(tok0, T_, state_only):
                    nb = T_ // L
                    for d in range(2):
                        blks = range(nb) if d == 0 else range(nb - 1, -1, -1)
                        rv = (lambda ap: ap) if d == 0 else (lambda ap: ap[:, ::-1])
                        for blk in blks:
                            uf, ufb = uf_r.next()
                            self.load(uf[:], cols_r[30 * 128:34 * 128, tok0 + blk * L: tok0 + (blk + 1) * L].rearrange("(c p) t -> p c t", p=128), ufb)
                            ub, ubb = ub_r.next()
                            self.copy("pool", ub[:].rearrange("p a b -> p (a b)"), uf[:].rearrange("p a b -> p (a b)"), [ufb], [ubb])
                            if not state_only:
                                py, pyb = py_r.next()
                            for sc in range(16):
                                i = d * 16 + sc
                                cu = sc // 4
                                tb_sl = slice(i * L, (i + 1) * L)
                                px, pxb = px_r.next()
                                op("pe", lambda e, px=px, i=i, cu=cu, ub=ub: e.matmul(px[:, 0:L], lhsT=Bp[:, 0, i, :], rhs=ub[:, cu, :], start=True, stop=True),
                                   [Bpb, ubb], [pxb])
                                op("pe", lambda e, px=px, i=i, cu=cu, ub=ub: e.matmul(px[:, L:2 * L], lhsT=Bp[:, 1, i, :], rhs=ub[:, cu, :], start=True, stop=True),
                                   [Bpb, ubb], [pxb])
                                X, Y = px[:, 0:L], px[:, L:2 * L]
                                t1, t1b = wk["t1"].next(); t2, t2b = wk["t2"].next()
                                wr, wrb = wk["wr"].next(); wi, wib = wk["wi"].next()
                                gr, grb = wk["gr"].next(); gi_, gib = wk["gi"].next()
                                T1v, T2v = rv(T1[:, tb_sl]), rv(T2[:, tb_sl])
                                CSv, SNv = rv(CS[:, tb_sl]), rv(SN[:, tb_sl])
                                op("dve", lambda e, t1=t1, X=X, T1v=T1v: e.tensor_tensor(out=t1[:], in0=X, in1=T1v, op=ALU.mult), [pxb, T1b], [t1b])
                                op("dve", lambda e, t2=t2, Y=Y, T2v=T2v: e.tensor_tensor(out=t2[:], in0=Y, in1=T2v, op=ALU.mult), [pxb, T2b], [t2b])
                                op("pool", lambda e, wr=wr, t1=t1, t2=t2: e.tensor_tensor(out=wr[:], in0=t1[:], in1=t2[:], op=ALU.subtract), [t1b, t2b], [wrb])
                                t1, t1b = wk["t1"].next(); t2, t2b = wk["t2"].next()
                                op("dve", lambda e, t1=t1, X=X, T2v=T2v: e.tensor_tensor(out=t1[:], in0=X, in1=T2v, op=ALU.mult), [pxb, T2b], [t1b])
                                op("dve", lambda e, t2=t2, Y=Y, T1v=T1v: e.tensor_tensor(out=t2[:], in0=Y, in1=T1v, op=ALU.mult), [pxb, T1b], [t2b])
                                op("pool", lambda e, wi=wi, t1=t1, t2=t2: e.tensor_tensor(out=wi[:], in0=t1[:], in1=t2[:], op=ALU.add), [t1b, t2b], [wib])
                                rbc = rm[:, i:i + 1].to_broadcast([128, L])
                                op("dve", lambda e, gr=gr, wr=wr, rbc=rbc, i=i: e.tensor_tensor_scan(
                                    out=rv(gr[:]), data0=rbc, data1=rv(wr[:]), initial=hin[:, i, 0:1], op0=ALU.mult, op1=ALU.add), [wrb, rmb, hinb], [grb])
                                op("dve", lambda e, gi_=gi_, wi=wi, rbc=rbc, i=i: e.tensor_tensor_scan(
                                    out=rv(gi_[:]), data0=rbc, data1=rv(wi[:]), initial=hin[:, i, 1:2], op0=ALU.mult, op1=ALU.add), [wib, rmb, hinb], [gib])
                                last = L - 1 if d == 0 else 0
                                cl, sl_ = CS[:, i * L + L - 1: i * L + L], SN[:, i * L + L - 1: i * L + L]
                                grl, gil = gr[:, last:last + 1], gi_[:, last:last + 1]
                                c1, c1b = wk["t1"].next()
                                op("dve", lambda e, c1=c1, grl=grl, cl=cl: e.tensor_tensor(out=c1[:, 0:1], in0=grl, in1=cl, op=ALU.mult), [grb, CSb], [c1b])
                                op("dve", lambda e, c1=c1, gil=gil, sl_=sl_: e.tensor_tensor(out=c1[:, 1:2], in0=gil, in1=sl_, op=ALU.mult), [gib, SNb, c1b], [c1b])
                                op("dve", lambda e, c1=c1, grl=grl, sl_=sl_: e.tensor_tensor(out=c1[:, 2:3], in0=grl, in1=sl_, op=ALU.mult), [grb, SNb, c1b], [c1b])
                                op("dve", lambda e, c1=c1, gil=gil, cl=cl: e.tensor_tensor(out=c1[:, 3:4], in0=gil, in1=cl, op=ALU.mult), [gib, CSb, c1b], [c1b])
                                op("dve", lambda e, c1=c1, i=i: e.tensor_tensor(out=hin[:, i, 0:1], in0=c1[:, 0:1], in1=c1[:, 1:2], op=ALU.subtract),
                                   [c1b], [hinb])
                                op("dve", lambda e, c1=c1, i=i: e.tensor_tensor(out=hin[:, i, 1:2], in0=c1[:, 2:3], in1=c1[:, 3:4], op=ALU.add),
                                   [c1b], [hinb])
                                if not state_only:
                                    hr_, hrb = wkh["hr"].next(); hi_, hib = wkh["hi"].next()
                                    t3, t3b = wk["t1"].next(); t4, t4b = wk["t2"].next()
                                    op("dve", lambda e, t3=t3, gr=gr, CSv=CSv: e.tensor_tensor(out=t3[:], in0=gr[:], in1=CSv, op=ALU.mult), [grb, CSb], [t3b])
                                    op("pool", lambda e, t4=t4, gi_=gi_, SNv=SNv: e.tensor_tensor(out=t4[:], in0=gi_[:], in1=SNv, op=ALU.mult), [gib, SNb], [t4b])
                                    op("pool", lambda e, hr_=hr_, t3=t3, t4=t4: e.tensor_tensor(out=hr_[:], in0=t3[:], in1=t4[:], op=ALU.subtract), [t3b, t4b], [hrb])
                                    t5, t5b = wk["t1"].next(); t6, t6b = wk["t2"].next()
                                    op("dve", lambda e, t5=t5, gr=gr, SNv=SNv: e.tensor_tensor(out=t5[:], in0=gr[:], in1=SNv, op=ALU.mult), [grb, SNb], [t5b])
                                    op("pool", lambda e, t6=t6, gi_=gi_, CSv=CSv: e.tensor_tensor(out=t6[:], in0=gi_[:], in1=CSv, op=ALU.mult), [gib, CSb], [t6b])
                                    op("dve", lambda e, hi_=hi_, t5=t5, t6=t6: e.scalar_tensor_tensor(out=hi_[:], in0=t5[:], scalar=-1.0, in1=t6[:],
                                                                                                     op0=ALU.mult, op1=ALU.subtract), [t5b, t6b], [hib])
                                    first = (sc % 4 == 0)
                                    lastmm = (sc % 4 == 3)
                                    op("pe", lambda e, py=py, hr_=hr_, i=i, cu=cu, first=first: e.matmul(
                                        py[:, cu * L:(cu + 1) * L], lhsT=Cp[:, 0, i, :], rhs=hr_[:], start=first, stop=False), [Cpb, hrb], [pyb])
                                    op("pe", lambda e, py=py, hi_=hi_, i=i, cu=cu, lastmm=lastmm: e.matmul(
                                        py[:, cu * L:(cu + 1) * L], lhsT=Cp[:, 1, i, :], rhs=hi_[:], start=False, stop=lastmm), [Cpb, hib], [pyb])
                            if not state_only:
                                ysl = yacc[:, :, blk * L:(blk + 1) * L]
                                pyv = py[:].rearrange("p (c t) -> p c t", t=L)
                                if d == 0:
                                    self.copy("act", ysl, pyv, [pyb], [yaccb])
                                else:
                                    op("dve", lambda e, ysl=ysl, pyv=pyv: e.tensor_tensor(out=ysl, in0=ysl, in1=pyv, op=ALU.add), [pyb, yaccb], [yaccb])

                def finish_seq(tok0, T_):
                    for cu in range(4):
                        for t0 in range(0, T_, 512):
                            tn = min(512, T_ - t0)
                            u_, u_b = sgt_r.next()
                            self.load(u_[:, 0:tn], cols_r[(30 + cu) * 128:(31 + cu) * 128, tok0 + t0: tok0 + t0 + tn], u_b)
                            ya = yacc[:, cu, t0:t0 + tn]
                            op("dve", lambda e, u_=u_, ya=ya, cu=cu, tn=tn: e.scalar_tensor_tensor(
                                out=ya, in0=u_[:, 0:tn], scalar=SPv("s5d", l * 4 + cu, 1), in1=ya, op0=ALU.mult, op1=ALU.add), [u_b, sp_b, yaccb], [yaccb])
                            op("dve", lambda e, u_=u_, ya=ya, tn=tn: e.tensor_tensor(out=u_[:, 0:tn], in0=ya, in1=ya, op=ALU.mult), [yaccb], [u_b])
                            op("dve", lambda e, u_=u_, tn=tn: e.tensor_scalar(out=u_[:, 0:tn], in0=u_[:, 0:tn], scalar1=0.044715, scalar2=1.0,
                                                                             op0=ALU.mult, op1=ALU.add), [u_b], [u_b])
                            op("dve", lambda e, u_=u_, ya=ya, tn=tn: e.tensor_tensor(out=u_[:, 0:tn], in0=u_[:, 0:tn], in1=ya, op=ALU.mult), [u_b, yaccb], [u_b])
                            op("act", lambda e, u_=u_, tn=tn: e.activation(out=u_[:, 0:tn], in_=u_[:, 0:tn], func=AF.Sigmoid,
                                                                          scale=2.0 * float(np.sqrt(2.0 / np.pi))), [u_b], [u_b])
                            op("dve", lambda e, u_=u_, ya=ya, cu=cu, t0=t0, tn=tn: e.tensor_tensor(out=ygl[:, cu, t0:t0 + tn], in0=u_[:, 0:tn], in1=ya,
                                                                                               op=ALU.mult), [u_b, yaccb], [yglb])
                    for t0 in range(0, T_, 512):
                        tn = min(512, T_ - t0)
                        for cc in range(4):
                            pz, pzb = pz_r.next()
                            pg, pgb = pg_r.next()
                            for kc in range(4):
                                op("pe", lambda e, pz=pz, kc=kc, cc=cc, t0=t0, tn=tn: e.matmul(
                                    pz[:, 0:tn], lhsT=glu_h[:, kc, cc * 128:(cc + 1) * 128], rhs=ygl[:, kc, t0:t0 + tn], start=(kc == 0), stop=(kc == 3)),
                                   [glu_hb, yglb], [pzb])
                            for kc in range(4):
                                op("pe", lambda e, pg=pg, kc=kc, cc=cc, t0=t0, tn=tn: e.matmul(
                                    pg[:, 0:tn], lhsT=glu_h[:, kc, 512 + cc * 128: 512 + (cc + 1) * 128], rhs=ygl[:, kc, t0:t0 + tn], start=(kc == 0),
                                    stop=(kc == 3)), [glu_hb, yglb], [pgb])
                            sg_, sg_b = sgt_r.next()
                            op("act", lambda e, sg_=sg_, pg=pg, tn=tn: e.activation(out=sg_[:, 0:tn], in_=pg[:, 0:tn], func=AF.Sigmoid), [pgb], [sg_b])
                            zo, zob = zo_r.next()
                            op("dve", lambda e, zo=zo, pz=pz, sg_=sg_, tn=tn: e.tensor_tensor(out=zo[:, 0:tn], in0=pz[:, 0:tn], in1=sg_[:, 0:tn], op=ALU.mult),
                               [pzb, sg_b], [zob])
                            self.store(ymix_r[(12 + cc) * 128:(13 + cc) * 128, tok0 + t0: tok0 + t0 + tn], zo[:, 0:tn], zob)

                hin_flat = hin[:].rearrange("p a b -> p (a b)")
                if mode == "pre":
                    op("pool", lambda e: e.memset(hin_flat, 0.0), [], [hinb])
                    run_seq(1024, 1024, True)
                    self.store(xblob[l].ap()[:, XO["s5"]:XO["s5"] + 64], hin_flat, hinb)
                else:
                    for sq_ in range(4):
                        op("pool", lambda e: e.memset(hin_flat, 0.0), [], [hinb])
                        run_seq(sq_ * 256, 256, False)
                        self.store(s5_out.ap()[l, sq_], hin_flat, hinb)
                        finish_seq(sq_ * 256, 256)
                    Bx, Bxb = self.T(ps, [128, 4, 32, 2], F32, "Bx")
                    self.load(Bx[:].rearrange("p r a b -> p r (a b)"), xg[:, :, XO["s5"]:XO["s5"] + 64], Bxb)
                    h0, h0b = self.T(ps, [128, 32, 2], F32, "h0")
                    self.load(h0[:].rearrange("p a b -> p (a b)"), s5h0_d.ap()[l], h0b)
                    ar_, arb = small("ar_"); ai_, aib = small("ai_"); q1, q1b = small("q1"); q2, q2b = small("q2")
                    op("act", lambda e: e.activation(out=q1[:], in_=er[:], func=AF.Exp, scale=float(L)), [erb], [q1b])
                    c128 = CS[:].rearrange("p (i t) -> p i t", t=L)[:, :, L - 1]
                    s128 = SN[:].rearrange("p (i t) -> p i t", t=L)[:, :, L - 1]
                    op("dve", lambda e: e.tensor_tensor(out=ar_[:], in0=q1[:], in1=c128, op=ALU.mult), [q1b, CSb], [arb])
                    op("dve", lambda e: e.tensor_tensor(out=ai_[:], in0=q1[:], in1=s128, op=ALU.mult), [q1b, SNb], [aib])
                    for _ in range(3):
                        op("dve", lambda e: e.tensor_tensor(out=q1[:], in0=ar_[:], in1=ar_[:], op=ALU.mult), [arb], [q1b])
                        op("dve", lambda e: e.tensor_tensor(out=q2[:], in0=ai_[:], in1=ai_[:], op=ALU.mult), [aib], [q2b])
                        op("dve", lambda e: e.tensor_tensor(out=q1[:], in0=q1[:], in1=q2[:], op=ALU.subtract), [q1b, q2b], [q1b])
                        op("dve", lambda e: e.tensor_tensor(out=q2[:], in0=ar_[:], in1=ai_[:], op=ALU.mult), [arb, aib], [q2b])
                        op("dve", lambda e: e.tensor_scalar(out=ai_[:], in0=q2[:], scalar1=2.0, scalar2=None, op0=ALU.mult), [q2b], [aib])
                        op("dve", lambda e: e.tensor_copy(out=ar_[:], in_=q1[:]), [q1b], [arb])
                    hc, hcb = self.T(ps, [128, 32, 2], F32, "hc")
                    hn, hnb = self.T(ps, [128, 32, 2], F32, "hn")
                    op("pool", lambda e: e.memset(hin_flat, 0.0), [], [hinb])
                    for d in range(2):
                        rg = slice(d * 16, (d + 1) * 16)
                        order = [0, 1, 2, 3] if d == 0 else [3, 2, 1, 0]
                        self.copy("dve", hc[:, rg, :], h0[:, rg, :], [h0b], [hcb])
                        for n_, j in enumerate(order):
                            op("dve", lambda e, rg=rg, j=j: e.scalar_tensor_tensor(out=hin[:, rg, :], in0=hc[:, rg, :], scalar=SPv("sel4", 8 + j, 1),
                                                                                  in1=hin[:, rg, :], op0=ALU.mult, op1=ALU.add), [hcb, sp_b, hinb], [hinb])
                            if n_ < 3:
                                cr, ci_ = hc[:, rg, 0], hc[:, rg, 1]
                                a_r, a_i = ar_[:, rg], ai_[:, rg]
                                op("dve", lambda e, rg=rg, cr=cr, a_r=a_r: e.tensor_tensor(out=hn[:, rg, 0], in0=cr, in1=a_r, op=ALU.mult), [hcb, arb], [hnb])
                                op("dve", lambda e, rg=rg, ci_=ci_, a_i=a_i: e.tensor_tensor(out=q1[:, rg], in0=ci_, in1=a_i, op=ALU.mult), [hcb, aib], [q1b])
                                op("dve", lambda e, rg=rg: e.tensor_tensor(out=hn[:, rg, 0], in0=hn[:, rg, 0], in1=q1[:, rg], op=ALU.subtract), [hnb, q1b], [hnb])
                                op("dve", lambda e, rg=rg, ci_=ci_, a_r=a_r: e.tensor_tensor(out=hn[:, rg, 1], in0=ci_, in1=a_r, op=ALU.mult), [hcb, arb, hnb], [hnb])
                                op("dve", lambda e, rg=rg, cr=cr, a_i=a_i: e.tensor_tensor(out=q1[:, rg], in0=cr, in1=a_i, op=ALU.mult), [hcb, aib], [q1b])
                                op("dve", lambda e, rg=rg: e.tensor_tensor(out=hn[:, rg, 1], in0=hn[:, rg, 1], in1=q1[:, rg], op=ALU.add), [hnb, q1b], [hnb])
                                op("dve", lambda e, rg=rg, j=j: e.tensor_tensor(out=hc[:, rg, :], in0=hn[:, rg, :], in1=Bx[:, j, rg, :], op=ALU.add),
                                   [hnb, Bxb], [hcb])
                    run_seq(1024, 1024, False)
                    finish_seq(1024, 1024)
                fw.barrier()

        khalo = self.dint("khalo", [128, 256])

        stop = self.stop

        def done(tag):
            return stop == tag

        def finish_dbg():
            fw.barrier()
            xsrc_dbg = x1.ap() if stop not in ("C0",) else xmid.ap()
            if stop[0] in "AB":
                xsrc_dbg = xin.ap()
            for o_, t_ in ((0, 0), (256, 1024)):
                self.d2d(dbg_y.ap()[:, o_:o_ + 256], ymixT.ap()[:, t_:t_ + 256])
                self.d2d(dbg_c.ap()[:, o_:o_ + 256], colsT.ap()[:, t_:t_ + 256])
                self.d2d(dbg_x.ap()[:, o_:o_ + 256], xsrc_dbg[:, t_:t_ + 256])
            fw.barrier()

        if not self.skipA:
            p1_mod()
            p0_layer(0)
            if stop is None or stop[-1] == "1":
                p0_layer(1)
        for l in range(2):
            xsrc = xin.ap() if l == 0 else x1.ap()
            xdst = x1.ap() if l == 0 else yT.ap()
            if not self.skipA:
                phase_A(l, xsrc)
            if done(f"A{l}"):
                return finish_dbg()
            att_prep(l)
            if done(f"Ba{l}"):
                return finish_dbg()
            try:
                hgrn_local(l)
            except Exception as _e:
                if type(_e).__name__ != "_Cut":
                    raise
                fw.barrier()
            if done(f"Bb{l}"):
                return finish_dbg()
            s5_phase(l, "pre")
            fw.barrier()
            fw.collective(G4, xblob[l].ap(), xgat[l].ap())
            fw.barrier()
            if done(f"Bc{l}"):
                return finish_dbg()
            pool_mixer(l)
            if done(f"Bd{l}"):
                return finish_dbg()
            attention(l)
            if done(f"Be{l}"):
                return finish_dbg()
            hgrn_post(l)
            if done(f"Bf{l}"):
                return finish_dbg()
            s5_phase(l, "main")
            if done(f"B{l}"):
                return finish_dbg()
            phase_C(l, xsrc, xmid.ap())
            if done(f"C{l}"):
                return finish_dbg()
            phase_D(l, xmid.ap(), xdst)
            if done(f"D{l}"):
                return finish_dbg()
        fw.barrier()


def _partner(m):
    d = m % 64
    base = m - d
    if d < 16:
        return base + d + 16
    if d < 32:
        return base + d - 16
    if d < 48:
        return base + d + 16
    return base + d - 16


def _consts_common():
    ct = np.zeros((128, CT_N), np.float32)
    ct[:, CT["ident"]:CT["ident"] + 128] = np.eye(128, dtype=np.float32)
    ct[:, CT["ones"]:CT["ones"] + 128] = 1.0
    k = np.arange(128)
    ct[:, CT["blk64"]:CT["blk64"] + 128] = (k[:, None] // 64 == k[None, :] // 64).astype(np.float32)
    perm = np.zeros((128, 128), np.float32)
    for m in range(128):
        perm[_partner(m), m] = 1.0
    ct[:, CT["perm"]:CT["perm"] + 128] = perm
    s = k[:, None]
    t = k[None, :]
    same = (s // 16 == t // 16)
    ct[:, CT["cmask"]:CT["cmask"] + 128] = (same & (s <= t)).astype(np.float32)
    ct[:, CT["cmask"] + 128:CT["cmask"] + 256] = (same & (s >= t)).astype(np.float32)
    ct[:, CT["bmask"]:CT["bmask"] + 8] = (k[:, None] // 16 == np.arange(8)[None, :]).astype(np.float32)
    ct[:, CT["iota1"]:CT["iota1"] + 128] = np.arange(1, 129, dtype=np.float32)[None, :]
    return ct


def _icnt(w, Tfull, pos):
    lo = np.clip(pos - w // 2, 0, Tfull)
    hi = np.clip(pos + w // 2, 0, Tfull)
    return (1.0 / (hi - lo)).astype(np.float32)


def _host_inputs(inp, c):
    b, s = c // 4, c % 4
    f32 = np.float32
    m = {}
    xp = np.asarray(inp["x_prompt"][4 * c:4 * c + 4], f32).reshape(1024, D)
    xs = np.asarray(inp["x_sample"][b, 1024 * s:1024 * (s + 1)], f32)
    m["xin"] = np.ascontiguousarray(np.concatenate([xp, xs], 0).T)
    m["w_in_s"] = np.ascontiguousarray(inp["w_in"][:, 256 * c:256 * (c + 1), :], f32)
    m["w_out_s"] = np.ascontiguousarray(inp["w_out"][:, 256 * c:256 * (c + 1), :], f32)
    m["w_up_s"] = np.ascontiguousarray(inp["ffn_w_up"][:, 256 * c:256 * (c + 1), :], f32)
    m["w_dn_s"] = np.ascontiguousarray(inp["ffn_w_down"][:, 704 * c:704 * (c + 1), :], f32)
    m["ada_s"] = np.ascontiguousarray(inp["ada_w"][:, :, 1536 * c:1536 * (c + 1)], f32)
    sp = np.zeros((128, SP_N), f32)

    def put(name, arr):
        arr = np.asarray(arr, f32)
        sp[:, SPO[name]:SPO[name] + arr.shape[1]] = arr

    rows = np.stack([np.asarray(inp["c_ctx"], f32), np.asarray(inp["c"][0], f32), np.asarray(inp["c"][1], f32)], 0)
    put("csT", rows.reshape(3, 16, 128).transpose(2, 1, 0).reshape(128, 48))
    put("adab", np.asarray(inp["ada_b"], f32)[:, 1536 * c:1536 * (c + 1)].reshape(2, 12, 128).transpose(2, 0, 1).reshape(128, 24))
    selb = np.zeros((128, 2), f32); selb[:, b] = 1.0
    put("selb", selb)
    sel4 = np.zeros((128, 12), f32)
    if s > 0:
        sel4[:, s - 1] = 1.0
    if s < 3:
        sel4[:, 4 + s + 1] = 1.0
    sel4[:, 8 + s] = 1.0
    put("sel4", sel4)
    e4 = np.ones((128, 4), f32); e4[:, 0] = float(s > 0); e4[:, 3] = float(s < 3)
    put("edge4", e4)
    pm = lambda a, nch: np.asarray(a, f32).reshape(-1, nch, 128)
    put("n1g", pm(inp["norm1_g"], 16).transpose(2, 0, 1).reshape(128, 32))
    put("n2g", pm(inp["norm2_g"], 16).transpose(2, 0, 1).reshape(128, 32))
    put("convw", np.asarray(inp["ffn_conv_w"], f32).reshape(2, 3, 44, 128).transpose(3, 0, 1, 2).reshape(128, 264))
    put("convb", np.asarray(inp["ffn_conv_b"], f32).reshape(2, 44, 128).transpose(2, 0, 1).reshape(128, 88))
    put("pool_scale", np.asarray(inp["pool_scale"], f32).reshape(2, 4, 128).transpose(2, 0, 1).reshape(128, 8))
    put("hg_lbraw", np.asarray(inp["hg_lb_raw"], f32).reshape(2, 2, 4, 128).transpose(3, 0, 1, 2).reshape(128, 16))
    put("hg_ng", np.asarray(inp["hg_norm_g"], f32).T)
    put("qng", np.tile(np.asarray(inp["q_norm_g"], f32).T, (2, 1)))
    put("kng", np.tile(np.asarray(inp["k_norm_g"], f32).T, (2, 1)))
    sk = np.asarray(inp["att_sink"], f32).reshape(2, 4, 2)
    put("sink", np.repeat(sk.transpose(2, 0, 1).reshape(2, 8), 64, axis=0))
    def s5lay(a):
        a = np.asarray(a, f32).reshape(2, 2, 16, 2, 64)
        return a.transpose(3, 4, 0, 1, 2).reshape(128, 64)
    put("s5are", s5lay(inp["s5_a_re"]))
    put("s5aim", s5lay(inp["s5_a_im"]))
    put("s5ldt", s5lay(np.repeat(np.asarray(inp["s5_log_dt"], f32)[..., None], 64, axis=-1)))
    put("s5d", np.asarray(inp["s5_d"], f32).reshape(2, 4, 128).transpose(2, 0, 1).reshape(128, 8))
    m["smallp"] = sp
    ct = _consts_common()
    j = np.arange(128)[:, None]
    i = np.arange(128)[None, :]
    am = np.zeros((4, 128, 128), f32)
    am[0] = (i <= j); am[1] = (j <= i)
    am[2] = am[0] * float(s > 0); am[3] = am[1] * float(s < 3)
    ct[:, CT["amask"]:CT["amask"] + 512] = am.transpose(1, 0, 2).reshape(128, 512)
    m["consts"] = ct
    t = np.arange(1024)
    hm = np.zeros((128, 2048), f32)
    hm[:, 0:1024] = (t % 16 != 0).astype(f32)[None]
    hm[:, 1024:] = (t % 16 != 15).astype(f32)[None]
    m["hmask"] = hm
    gpos = 1024 * s + t
    rowp, colp = gpos // 64, gpos % 64
    inv = (10000.0 ** (-np.arange(16, dtype=np.float32) / 16)).astype(f32)
    rope = np.zeros((128, 2048), f32)
    for p in range(128):
        d = p % 64
        pos = rowp if d < 32 else colp
        ang = pos.astype(f32) * inv[d % 16]
        rope[p, 0:1024] = np.cos(ang)
        sn = np.sin(ang)
        rope[p, 1024:] = -sn if (d % 32) < 16 else sn
    m["rope"] = rope
    pic = np.zeros((128, 5120), f32)
    for gi, w in enumerate((2, 4, 8, 16)):
        pic[:, gi * 256:(gi + 1) * 256] = _icnt(w, 256, np.arange(256))[None]
        pic[:, 1024 + gi * 1024: 1024 + (gi + 1) * 1024] = _icnt(w, 4096, gpos)[None]
    m["pic"] = pic
    m["pool_w"] = np.ascontiguousarray(inp["pool_w"], f32)
    m["s5_glu"] = np.ascontiguousarray(inp["s5_w_glu"], f32)
    bre = np.asarray(inp["s5_b_re"], f32).transpose(0, 1, 2, 4, 3)
    bim = np.asarray(inp["s5_b_im"], f32).transpose(0, 1, 2, 4, 3)
    m["s5b"] = np.ascontiguousarray(np.stack([bre, bim], 2))
    cre = np.asarray(inp["s5_c_re"], f32).transpose(0, 1, 2, 4, 3)
    cim = np.asarray(inp["s5_c_im"], f32).transpose(0, 1, 2, 4, 3)
    m["s5c"] = np.ascontiguousarray(np.stack([cre, cim], 2))
    hre = np.asarray(inp["state_s5_re"][b], f32).reshape(2, 2, 16, 2, 64)
    him = np.asarray(inp["state_s5_im"][b], f32).reshape(2, 2, 16, 2, 64)
    h0 = np.stack([hre, him], -1)
    m["s5h0"] = np.ascontiguousarray(h0.transpose(0, 3, 4, 1, 2, 5).reshape(2, 128, 64))
    m["hg0"] = np.ascontiguousarray(inp["state_hgrn"][b], f32)
    ck = np.asarray(inp["cache_k"][b], f32).transpose(0, 2, 3, 1)
    m["ckT"] = np.ascontiguousarray(np.concatenate([ck, ck], 2))
    m["cv"] = np.ascontiguousarray(np.asarray(inp["cache_v"][b], f32).reshape(2, 256, 128))
    return m


_PROG = {}


def _get_prog(stop=None, skipA=False):
    if stop not in _PROG:
        p = Prog(stop, skipA)
        p.build()
        _PROG[stop] = p
    return _PROG[stop]


def _run(inputs, stop=None, skipA=False):
    p = _get_prog(stop, skipA)
    in_maps = [_host_inputs(inputs, c) for c in range(8)]
    if skipA:
        for m in in_maps:
            for k in ("w_in_s", "w_out_s", "w_up_s", "w_dn_s", "ada_s"):
                m.pop(k, None)
    if p.NL == 1 and not skipA:
        for m in in_maps:
            for k in ("w_in_s", "w_out_s", "w_up_s", "w_dn_s", "ada_s"):
                m[k] = np.ascontiguousarray(m[k][0:1])
    res = run_bass_kernel_spmd(p.nc, in_maps, core_ids=list(range(8)))
    return res.results


def kernel(**inputs):
    r = _run(inputs)
    f32 = np.float32
    y_prompt = np.zeros((32, 256, D), f32)
    y_sample = np.zeros((2, 4096, D), f32)
    nck = np.zeros((32, 2, 256, 2, 64), f32)
    ncv = np.zeros((32, 2, 256, 2, 64), f32)
    nhg = np.zeros((32, 2, 2, 4, 128, 128), f32)
    ns5r = np.zeros((32, 2, 2, 32, 64), f32)
    ns5i = np.zeros((32, 2, 2, 32, 64), f32)
    for c in range(8):
        b, s = c // 4, c % 4
        yT = np.asarray(r[c]["yT"])
        y_prompt[4 * c:4 * c + 4] = yT[:, 0:1024].T.reshape(4, 256, D)
        y_sample[b, 1024 * s:1024 * (s + 1)] = yT[:, 1024:].T
        ck = np.asarray(r[c]["ck_out"]).reshape(2, 4, 2, 64, 256)
        cv = np.asarray(r[c]["cv_out"]).reshape(2, 4, 2, 64, 256)
        nck[4 * c:4 * c + 4] = ck.transpose(1, 0, 4, 2, 3)
        ncv[4 * c:4 * c + 4] = cv.transpose(1, 0, 4, 2, 3)
        nhg[4 * c:4 * c + 4] = np.asarray(r[c]["hg_out"]).transpose(1, 0, 2, 3, 4, 5)
        s5 = np.asarray(r[c]["s5_out"]).reshape(2, 4, 2, 64, 2, 16, 2)
        s5 = s5.transpose(1, 0, 4, 5, 2, 3, 6).reshape(4, 2, 2, 32, 64, 2)
        ns5r[4 * c:4 * c + 4] = s5[..., 0]
        ns5i[4 * c:4 * c + 4] = s5[..., 1]
    return (y_prompt, y_sample, nck, ncv, nhg, ns5r, ns5i)
```
